# Optimizing a Trainium2 kernel written in Bass

```python
import jax, jax.numpy as jnp
from jax import lax
import numpy as np

D_MODEL = 1024
BATCH = 8
SEQ = 4096
DEPTH = 2

GRID_W = 64
CTX_LEN = 256
EPS = 1e-6
N_MOD = 9
D_FF = 2816
BRANCH_W = 512
N_BRANCH = 3
MLA_HEADS = 8
MLA_Q_RANK = 256
MLA_KV_RANK = 128
MLA_NOPE = 64
MLA_ROPE = 32
MLA_V = 64
MLA_QK = MLA_NOPE + MLA_ROPE
AXIS_DIM = MLA_ROPE // 2
ROPE_THETA = 10000.0
Q_BLOCK = 128
POOL_WINDOWS = (2, 4, 8, 16)
POOL_GROUPS = len(POOL_WINDOWS)
POOL_GDIM = BRANCH_W // POOL_GROUPS
GLA_HEADS = 4
GLA_DK = 64
GLA_DV = 128
GLA_GATE_RANK = 16
GLA_TAU = 16.0
GLA_CHUNK = 64
IN_SPLITS = (MLA_Q_RANK, MLA_KV_RANK, MLA_ROPE,
             BRANCH_W,
             GLA_HEADS * GLA_DK, GLA_HEADS * GLA_DK,
             GLA_HEADS * GLA_DV,
             2 * GLA_GATE_RANK,
             GLA_HEADS * GLA_DV,
             N_BRANCH * D_MODEL)
D_IN = sum(IN_SPLITS)

kernel_name = "hybrid_mla_pool_gla_macaron_dit"

F32 = jnp.float32


def rmsnorm(x, g):
    xf = x.astype(F32)
    y = xf * lax.rsqrt(jnp.mean(xf * xf, axis=-1, keepdims=True) + EPS)
    return (y * g.astype(F32)).astype(x.dtype)


def modulate(x, shift, scale):
    return x * (1 + scale) + shift


def ffn_half_step(h, g, shift, scale, gate, w1, w3, w2):
    u = modulate(rmsnorm(h, g), shift, scale)
    return h + 0.5 * gate * ((jax.nn.silu(u @ w1) * (u @ w3)) @ w2)


def split_cols(z):
    idx = [int(i) for i in np.cumsum(IN_SPLITS)[:-1]]
    return jnp.split(z, idx, axis=-1)


def rope_tables(L):
    rows = L // GRID_W
    r, col = jnp.meshgrid(jnp.arange(rows, dtype=F32), jnp.arange(GRID_W, dtype=F32), indexing="ij")
    inv = ROPE_THETA ** (-jnp.arange(0, AXIS_DIM, 2, dtype=F32) / AXIS_DIM)
    pos = jnp.stack([r.reshape(-1), col.reshape(-1)], axis=-1)
    ang = pos[:, :, None] * inv
    return jnp.cos(ang), jnp.sin(ang)


def apply_rope2d(x, cos, sin):
    shp = x.shape
    xf = x.astype(F32).reshape(shp[:-1] + (2, 2, AXIS_DIM // 2))
    bshape = (shp[1],) + (1,) * (x.ndim - 3) + (2, AXIS_DIM // 2)
    cs, sn = cos.reshape(bshape), sin.reshape(bshape)
    x1, x2 = xf[..., 0, :], xf[..., 1, :]
    out = jnp.stack([x1 * cs - x2 * sn, x2 * cs + x1 * sn], axis=-2)
    return out.reshape(shp).astype(x.dtype)


def mla_project(cq, ckv, kr, g_cq, w_uq, g_ckv, w_ukv, g_qn, g_kn, rope):
    B, L = cq.shape[:2]
    q = (rmsnorm(cq, g_cq) @ w_uq).reshape(B, L, MLA_HEADS, MLA_QK)
    kv = (rmsnorm(ckv, g_ckv) @ w_ukv).reshape(B, L, MLA_HEADS, MLA_NOPE + MLA_V)
    k = jnp.concatenate([kv[..., :MLA_NOPE],
                         jnp.broadcast_to(kr[:, :, None, :], (B, L, MLA_HEADS, MLA_ROPE))], axis=-1)
    q, k = rmsnorm(q, g_qn), rmsnorm(k, g_kn)
    if rope is not None:
        cos, sin = rope
        q = jnp.concatenate([q[..., :MLA_NOPE], apply_rope2d(q[..., MLA_NOPE:], cos, sin)], axis=-1)
        k = jnp.concatenate([k[..., :MLA_NOPE], apply_rope2d(k[..., MLA_NOPE:], cos, sin)], axis=-1)
    return q, k, kv[..., MLA_NOPE:]


def attend(q, k, v):
    s = jnp.einsum("bqhd,bkhd->bhqk", q.astype(F32), k.astype(F32)) * (MLA_QK ** -0.5)
    p = jax.nn.softmax(s, axis=-1)
    return jnp.einsum("bhqk,bkhd->bqhd", p, v.astype(F32)).astype(v.dtype)


def blocked_attend(q, k, v):
    B, L, H, Dq = q.shape
    nb = L // Q_BLOCK
    qb = q.reshape(B, nb, Q_BLOCK, H, Dq).transpose(1, 0, 2, 3, 4)
    out = lax.map(lambda qi: attend(qi, k, v), qb)
    return out.transpose(1, 0, 2, 3, 4).reshape(B, L, H, v.shape[-1])


def multiscale_pool(p, w_pool, pool_scale):
    B, L, _ = p.shape
    pf = p.astype(F32)
    S = jnp.concatenate([jnp.zeros((B, 1, BRANCH_W), F32), jnp.cumsum(pf, axis=1)], axis=1)
    t = jnp.arange(L)
    outs = []
    for g, w in enumerate(POOL_WINDOWS):
        lo = jnp.clip(t - w // 2, 0, L)
        hi = jnp.clip(t + w // 2, 0, L)
        sl = slice(g * POOL_GDIM, (g + 1) * POOL_GDIM)
        Sg = S[..., sl]
        mean = (Sg[:, hi] - Sg[:, lo]) / (hi - lo).astype(F32)[None, :, None]
        outs.append(mean - pf[..., sl])
    pooled = jnp.stack(outs, axis=2).astype(p.dtype)
    y = jnp.einsum("blgc,gcd->blgd", pooled, w_pool).reshape(B, L, BRANCH_W)
    return y * pool_scale


def gla_prepare(q, k, v, a_lr, w_a2, b_a2):
    B, L = q.shape[:2]
    qh = q.astype(F32).reshape(B, L, GLA_HEADS, GLA_DK) * (GLA_DK ** -0.5)
    kh = k.astype(F32).reshape(B, L, GLA_HEADS, GLA_DK)
    vh = v.astype(F32).reshape(B, L, GLA_HEADS, GLA_DV)
    logit = jnp.einsum("bldr,drk->bldk", a_lr.reshape(B, L, 2, GLA_GATE_RANK), w_a2) + b_a2
    log_a = (jax.nn.log_sigmoid(logit.astype(F32)) / GLA_TAU).reshape(B, L, 2, GLA_HEADS, GLA_DK)
    return qh, kh, vh, log_a


def gla_chunk_scan(q, k, v, log_a, s0):
    B, L, H, DK = q.shape
    DV = v.shape[-1]
    N, C = L // GLA_CHUNK, GLA_CHUNK

    def chunks(t):
        return t.reshape(B, N, C, H, t.shape[-1]).transpose(0, 3, 1, 2, 4)

    qc, kc, vc, ac = chunks(q), chunks(k), chunks(v), chunks(log_a)
    b = jnp.cumsum(ac, axis=3)
    b_last = b[:, :, :, -1:, :]
    q_t = qc * jnp.exp(b)
    k_t = kc * jnp.exp(-b)
    k_end = kc * jnp.exp(b_last - b)
    mask = jnp.tril(jnp.ones((C, C), dtype=bool))
    A = jnp.where(mask, jnp.einsum("bhnik,bhnjk->bhnij", q_t, k_t), 0.0)
    o_intra = jnp.einsum("bhnij,bhnjv->bhniv", A, vc)
    dS = jnp.einsum("bhnjk,bhnjv->nbhkv", k_end, vc)
    decay = jnp.exp(b_last[:, :, :, 0, :]).transpose(2, 0, 1, 3)

    def step(S, inp):
        d, ds = inp
        return d[..., None] * S + ds, S

    s_final, s_starts = lax.scan(step, s0, (decay, dS))
    o_inter = jnp.einsum("bhnik,nbhkv->bhniv", q_t, s_starts)
    o = (o_intra + o_inter).transpose(0, 2, 3, 1, 4).reshape(B, L, H, DV)
    return o, s_final


def gla_bidir(q, k, v, log_a, s0_f, s0_b):
    o_f, s_f = gla_chunk_scan(q, k, v, log_a[:, :, 0], s0_f)
    fl = lambda t: jnp.flip(t, axis=1)
    o_b, s_b = gla_chunk_scan(fl(q), fl(k), fl(v), fl(log_a[:, :, 1]), s0_b)
    return o_f + fl(o_b), s_f, s_b


def gla_output(o, r, g):
    B, L = o.shape[:2]
    on = rmsnorm(o, g).reshape(B, L, GLA_HEADS * GLA_DV).astype(r.dtype)
    return on * jax.nn.silu(r)


def merge_branches(gates, o_a, o_p, o_g, w_branch, w_out):
    B, L = gates.shape[:2]
    g = jax.nn.sigmoid(gates.astype(F32)).astype(gates.dtype).reshape(B, L, N_BRANCH, D_MODEL)
    m = (g[:, :, 0] * (o_a @ w_branch[0])
         + g[:, :, 1] * (o_p @ w_branch[1])
         + g[:, :, 2] * (o_g @ w_branch[2]))
    return m @ w_out


def token_mixer(u, uc, rope, w_in, g_cq, w_uq, g_ckv, w_ukv, g_qn, g_kn, w_pool, pool_scale,
                w_a2, b_a2, g_gla_o, w_branch, w_out, need_ctx):
    B, L, _ = u.shape
    cq, ckv, kr, pz, gq, gk, gv, ga, gr, gates = split_cols(u @ w_in)
    cq_c, ckv_c, kr_c, pz_c, gq_c, gk_c, gv_c, ga_c, gr_c, gates_c = split_cols(uc @ w_in)
    q, k, v = mla_project(cq, ckv, kr, g_cq, w_uq, g_ckv, w_ukv, g_qn, g_kn, rope)
    q_c, k_c, v_c = mla_project(cq_c, ckv_c, kr_c, g_cq, w_uq, g_ckv, w_ukv, g_qn, g_kn, None)
    k_all = jnp.concatenate([k_c, k], axis=1)
    v_all = jnp.concatenate([v_c, v], axis=1)
    o_a = blocked_attend(q, k_all, v_all).reshape(B, L, MLA_HEADS * MLA_V)
    o_p = multiscale_pool(pz, w_pool, pool_scale)
    qg, kg, vg, la = gla_prepare(gq, gk, gv, ga, w_a2, b_a2)
    qg_c, kg_c, vg_c, la_c = gla_prepare(gq_c, gk_c, gv_c, ga_c, w_a2, b_a2)
    s0 = jnp.zeros((uc.shape[0], GLA_HEADS, GLA_DK, GLA_DV), F32)
    og_c, s_f, s_b = gla_bidir(qg_c, kg_c, vg_c, la_c, s0, s0)
    og, _, _ = gla_bidir(qg, kg, vg, la, s_f, s_b)
    o_g = gla_output(og, gr, g_gla_o)
    y = merge_branches(gates, o_a, o_p, o_g, w_branch, w_out)
    if not need_ctx:
        return y, None
    Lc = uc.shape[1]
    o_a_c = attend(q_c, k_c, v_c).reshape(uc.shape[0], Lc, MLA_HEADS * MLA_V)
    o_p_c = multiscale_pool(pz_c, w_pool, pool_scale)
    o_g_c = gla_output(og_c, gr_c, g_gla_o)
    yc = merge_branches(gates_c, o_a_c, o_p_c, o_g_c, w_branch, w_out)
    return y, yc


def setup_inputs(seed: int = 0) -> dict:
    key = jax.random.key(seed)
    it = iter(jax.random.split(key, 40))

    def nrm(shape, scale):
        return jax.random.normal(next(it), shape, F32) * scale

    def gain(shape):
        return 1.0 + 0.1 * jax.random.normal(next(it), shape, F32)

    D = D_MODEL
    return {
        "x": nrm((BATCH, SEQ, D), 1.0),
        "c": nrm((BATCH, D), 1.0),
        "ctx": nrm((BATCH, CTX_LEN, D), 1.0),
        "c_ctx": nrm((D,), 1.0),
        "w_ada": nrm((DEPTH, D, N_MOD * D), 0.5 * D ** -0.5),
        "b_ada": nrm((DEPTH, N_MOD * D), 0.02),
        "g_ffn1": gain((DEPTH, D)),
        "ffn1_w1": nrm((DEPTH, D, D_FF), D ** -0.5),
        "ffn1_w3": nrm((DEPTH, D, D_FF), D ** -0.5),
        "ffn1_w2": nrm((DEPTH, D_FF, D), D_FF ** -0.5),
        "g_mix": gain((DEPTH, D)),
        "w_in": nrm((DEPTH, D, D_IN), D ** -0.5),
        "g_cq": gain((DEPTH, MLA_Q_RANK)),
        "w_uq": nrm((DEPTH, MLA_Q_RANK, MLA_HEADS * MLA_QK), MLA_Q_RANK ** -0.5),
        "g_ckv": gain((DEPTH, MLA_KV_RANK)),
        "w_ukv": nrm((DEPTH, MLA_KV_RANK, MLA_HEADS * (MLA_NOPE + MLA_V)), MLA_KV_RANK ** -0.5),
        "g_qn": gain((DEPTH, MLA_QK)),
        "g_kn": gain((DEPTH, MLA_QK)),
        "w_pool": nrm((DEPTH, POOL_GROUPS, POOL_GDIM, POOL_GDIM), POOL_GDIM ** -0.5),
        "pool_scale": gain((DEPTH, BRANCH_W)),
        "w_a2": nrm((DEPTH, 2, GLA_GATE_RANK, GLA_HEADS * GLA_DK), GLA_GATE_RANK ** -0.5),
        "b_a2": nrm((DEPTH, 2, GLA_HEADS * GLA_DK), 0.5),
        "g_gla_o": gain((DEPTH, GLA_DV)),
        "w_branch": nrm((DEPTH, N_BRANCH, BRANCH_W, D), BRANCH_W ** -0.5),
        "w_out": nrm((DEPTH, D, D), D ** -0.5),
        "g_ffn2": gain((DEPTH, D)),
        "ffn2_w1": nrm((DEPTH, D, D_FF), D ** -0.5),
        "ffn2_w3": nrm((DEPTH, D, D_FF), D ** -0.5),
        "ffn2_w2": nrm((DEPTH, D_FF, D), D_FF ** -0.5),
    }


def reference(x, c, ctx, c_ctx, w_ada, b_ada, g_ffn1, ffn1_w1, ffn1_w3, ffn1_w2, g_mix, w_in,
              g_cq, w_uq, g_ckv, w_ukv, g_qn, g_kn, w_pool, pool_scale, w_a2, b_a2, g_gla_o,
              w_branch, w_out, g_ffn2, ffn2_w1, ffn2_w3, ffn2_w2):
    B, L, D = x.shape
    rope = rope_tables(L)
    h, hc = x, ctx
    sc, scc = jax.nn.silu(c), jax.nn.silu(c_ctx)
    for l in range(DEPTH):
        last = l == DEPTH - 1
        mod = (sc @ w_ada[l] + b_ada[l]).reshape(B, 1, N_MOD, D)
        mod_c = (scc @ w_ada[l] + b_ada[l]).reshape(1, 1, N_MOD, D)
        h = ffn_half_step(h, g_ffn1[l], mod[:, :, 0], mod[:, :, 1], mod[:, :, 2],
                          ffn1_w1[l], ffn1_w3[l], ffn1_w2[l])
        hc = ffn_half_step(hc, g_ffn1[l], mod_c[:, :, 0], mod_c[:, :, 1], mod_c[:, :, 2],
                           ffn1_w1[l], ffn1_w3[l], ffn1_w2[l])
        u = modulate(rmsnorm(h, g_mix[l]), mod[:, :, 3], mod[:, :, 4])
        uc = modulate(rmsnorm(hc, g_mix[l]), mod_c[:, :, 3], mod_c[:, :, 4])
        y, yc = token_mixer(u, uc, rope, w_in[l], g_cq[l], w_uq[l], g_ckv[l], w_ukv[l], g_qn[l],
                            g_kn[l], w_pool[l], pool_scale[l], w_a2[l], b_a2[l], g_gla_o[l],
                            w_branch[l], w_out[l], not last)
        h = h + mod[:, :, 5] * y
        h = ffn_half_step(h, g_ffn2[l], mod[:, :, 6], mod[:, :, 7], mod[:, :, 8],
                          ffn2_w1[l], ffn2_w3[l], ffn2_w2[l])
        if not last:
            hc = hc + mod_c[:, :, 5] * yc
            hc = ffn_half_step(hc, g_ffn2[l], mod_c[:, :, 6], mod_c[:, :, 7], mod_c[:, :, 8],
                               ffn2_w1[l], ffn2_w3[l], ffn2_w2[l])
    return h
```

```python
import numpy as np
import concourse.bass as bass
import concourse.mybir as mybir
from concourse.bass_utils import run_bass_kernel_spmd

F32 = mybir.dt.float32
BF16 = mybir.dt.bfloat16
AF = mybir.ActivationFunctionType
ALU = mybir.AluOpType


class Buf:
    __slots__ = ("name", "w", "r")

    def __init__(self, name):
        self.name = name
        self.w = None
        self.r = {}


class Op:
    __slots__ = ("eng", "fn", "chan", "idx", "waits", "clock", "sig", "dma")


class Sched:
    NDMA = 8

    def __init__(self, nc):
        self.nc = nc
        self.engs = {"pe": nc.tensor, "act": nc.scalar, "dve": nc.vector, "pool": nc.gpsimd, "sp": nc.sync}
        self.ops = {e: [] for e in self.engs}
        self.cnt = {}
        self.known = {e: {} for e in self.engs}
        self.dma_rr = {e: 0 for e in self.engs}
        self.dma_last = {}
        self.all_ops = []
        self.last = {}
        self.pending = {e: [] for e in self.engs}

    def barrier(self):
        lasts = list(self.last.values())
        for e in self.engs:
            self.pending[e] = list(lasts)

    def op(self, eng, fn, reads=(), writes=(), dma=False):
        o = Op()
        o.eng, o.fn, o.dma, o.sig = eng, fn, dma, dma
        k = self.known[eng]
        need = {}
        objs = []

        def add(d):
            if d is None or k.get(d.chan, 0) >= d.idx:
                return
            if need.get(d.chan, 0) < d.idx:
                need[d.chan] = d.idx
            objs.append(d)

        for d in self.pending[eng]:
            add(d)
        self.pending[eng] = []
        for b in reads:
            add(b.w)
        for b in writes:
            for d in [b.w] + list(b.r.values()):
                if d is None:
                    continue
                if d.chan == eng and not dma and eng == "pe":
                    continue
                add(d)
        if dma:
            q = self.dma_rr[eng]
            self.dma_rr[eng] = (q + 1) % self.NDMA
            o.chan = f"d_{eng}_{q}"
            add(self.dma_last.get(o.chan))
        else:
            o.chan = eng
        o.waits = list(need.items())
        for chan, idx in o.waits:
            k[chan] = max(k.get(chan, 0), idx)
        for d in objs:
            for c2, i2 in d.clock.items():
                if k.get(c2, 0) < i2:
                    k[c2] = i2
        self.cnt[o.chan] = self.cnt.get(o.chan, 0) + 1
        o.idx = self.cnt[o.chan]
        o.clock = dict(k)
        o.clock[o.chan] = o.idx
        if dma:
            self.dma_last[o.chan] = o
        for b in reads:
            b.r[o.chan] = o
        for b in writes:
            b.w = o
            b.r = {}
        self.last[o.chan] = o
        self.ops[eng].append(o)
        self.all_ops.append(o)
        return o

    def finish(self, final_ops):
        nc = self.nc
        byid = {}
        for e, lst in self.ops.items():
            for o in lst:
                byid[(o.chan, o.idx)] = o
        for e, lst in self.ops.items():
            for o in lst:
                for w in o.waits:
                    byid[w].sig = True
        for o in final_ops:
            o.sig = True
        val = {}
        run = {}
        for o in self.all_ops:
            if o.dma:
                val[(o.chan, o.idx)] = 16 * o.idx
        for e, lst in self.ops.items():
            c = 0
            for o in lst:
                if not o.dma:
                    if o.sig:
                        c += 1
                    val[(o.chan, o.idx)] = c if o.sig else None
        chans = sorted(self.cnt.keys())
        sems = {c: nc.alloc_semaphore(name=f"s_{c}") for c in chans}
        self.maxval = max([v for v in val.values() if v is not None] + [0])
        with nc.Block() as block:
            def emit(eng_name):
                def body(eng):
                    for o in self.ops[eng_name]:
                        for w in o.waits:
                            eng.wait_ge(sems[w[0]], val[w])
                        ins = o.fn(eng)
                        if o.sig:
                            ins.then_inc(sems[o.chan], 16 if o.dma else 1)
                    for o in final_ops:
                        if o.eng == eng_name:
                            eng.wait_ge(sems[o.chan], val[(o.chan, o.idx)])
                return body
            for name in self.engs:
                if self.ops[name]:
                    getattr(block, {"pe": "tensor", "act": "scalar", "dve": "vector", "pool": "gpsimd", "sp": "sync"}[name])(emit(name))


D = 1024
L = 4096
LC = 256
T = L + LC
FF = 2816
NFF = FF // 128
DEPTH = 2
EPS = 1.0000001e-6
TILES = [(0, 256)] + [(256 + 512 * i, 512) for i in range(8)]
DIN = 5568


class TT:
    __slots__ = ("ap", "buf")

    def __init__(self, ap, buf):
        self.ap, self.buf = ap, buf

    def __getitem__(self, key):
        return TT(self.ap[key], self.buf)

    def re(self, pattern, **kw):
        return TT(self.ap.rearrange(pattern, **kw), self.buf)


class DT:
    def __init__(self, ap, name):
        self.ap, self.name, self.bufs = ap, name, {}

    def t(self, key, ap=None):
        if key not in self.bufs:
            self.bufs[key] = Buf(f"{self.name}_{key}")
        return TT(self.ap if ap is None else ap, self.bufs[key])


class Builder:
    ARENA_WORDS = 52000

    def __init__(self):
        self.nc = nc = bass.Bass("TRN2", target_bir_lowering=False)
        self.S = Sched(nc)
        self.arena = nc.alloc_sbuf_tensor("arena", [128, self.ARENA_WORDS], F32)
        self.off = 0
        self.base = 0
        self.psum = [TT(nc.alloc_psum_tensor(f"ps{i}", [128, 512], F32)[:, :], Buf(f"ps{i}")) for i in range(8)]
        self.nbuf = 0

    def alloc(self, n, dtype=F32, name=None):
        words = (n * (2 if dtype == BF16 else 4) + 3) // 4
        assert self.off + words <= self.ARENA_WORDS, (self.off, words)
        ap = self.arena[:, self.off:self.off + words]
        self.off += words
        if dtype == BF16:
            ap = ap.bitcast(BF16)[:, 0:n]
        self.nbuf += 1
        return TT(ap, Buf(name or f"b{self.nbuf}"))

    def persist(self):
        self.base = self.off

    def new_stage(self):
        self.S.barrier()
        self.off = self.base

    def dram_in(self, name, shape):
        return DT(self.nc.dram_tensor(name, list(shape), F32, kind="ExternalInput").ap(), name)

    def dram_tmp(self, name, shape, dtype):
        return DT(self.nc.dram_tensor(name, list(shape), dtype).ap(), name)

    def dma(self, q, out, in_):
        return self.S.op(q, lambda e: e.dma_start(out=out.ap, in_=in_.ap), [in_.buf], [out.buf], dma=True)

    def mm(self, out, lhsT, rhs, start=True, stop=True):
        return self.S.op("pe", lambda e: e.matmul(out.ap, lhsT.ap, rhs.ap, start=start, stop=stop),
                         [lhsT.buf, rhs.buf], [out.buf])

    def tr(self, out, in_, ident):
        return self.S.op("pe", lambda e: e.transpose(out.ap, in_.ap, ident.ap), [in_.buf, ident.buf], [out.buf])

    def act(self, out, in_, func, bias=None, scale=None, eng="act"):
        rd = [in_.buf]
        kw = {}
        if bias is not None:
            if isinstance(bias, TT):
                rd.append(bias.buf); kw["bias"] = bias.ap
            else:
                kw["bias"] = bias
        if scale is not None:
            if isinstance(scale, TT):
                rd.append(scale.buf); kw["scale"] = scale.ap
            else:
                kw["scale"] = scale
        return self.S.op("act", lambda e: e.activation(out.ap, in_.ap, func, **kw), rd, [out.buf])

    def copy(self, eng, out, in_):
        if eng == "act":
            return self.S.op(eng, lambda e: e.activation(out.ap, in_.ap, AF.Copy), [in_.buf], [out.buf])
        return self.S.op(eng, lambda e: e.tensor_copy(out.ap, in_.ap), [in_.buf], [out.buf])

    def tt(self, eng, out, in0, in1, op):
        return self.S.op(eng, lambda e: e.tensor_tensor(out.ap, in0.ap, in1.ap, op), [in0.buf, in1.buf], [out.buf])

    def ts(self, eng, out, in0, s1, s2, op0, op1=None):
        rd = [in0.buf]
        a1 = s1
        a2 = s2
        if isinstance(s1, TT):
            rd.append(s1.buf); a1 = s1.ap
        if isinstance(s2, TT):
            rd.append(s2.buf); a2 = s2.ap
        if op1 is None:
            return self.S.op(eng, lambda e: e.tensor_scalar(out.ap, in0.ap, a1, None, op0), rd, [out.buf])
        return self.S.op(eng, lambda e: e.tensor_scalar(out.ap, in0.ap, a1, a2, op0, op1), rd, [out.buf])

    def stt(self, eng, out, in0, scalar, in1, op0, op1):
        rd = [in0.buf, in1.buf]
        sc = scalar
        if isinstance(scalar, TT):
            rd.append(scalar.buf); sc = scalar.ap
        return self.S.op(eng, lambda e: e.scalar_tensor_tensor(out.ap, in0.ap, sc, in1.ap, op0, op1), rd, [out.buf])

    def memset(self, eng, out, v):
        return self.S.op(eng, lambda e: e.memset(out.ap, v), [], [out.buf])


NVL = 36
WNAMES = ["w_ada", "ffn1_w1", "ffn1_w3", "ffn1_w2", "w_in", "w_uq", "w_uq_rot", "w_ukv", "w_pool", "w_a2", "w_branch",
          "w_out", "ffn2_w1", "ffn2_w3", "ffn2_w2"]


class Model:
    def __init__(self, cfg):
        self.cfg = cfg
        self.B = B = Builder()
        self.nc = B.nc
        nc = B.nc
        self.x_in = B.dram_in("x", [L, D])
        self.ctx_in = B.dram_in("ctx", [LC, D])
        self.cvec = B.dram_in("cvec", [128, 16])
        self.vecs_in = B.dram_in("vecs", [128, DEPTH * NVL])
        self.bada_in = B.dram_in("bada", [128, DEPTH * 72])
        self.idn_in = B.dram_in("idn", [128, 128])
        self.W = {}
        shapes = {"w_ada": [DEPTH, D, 9 * D], "ffn1_w1": [DEPTH, D, FF], "ffn1_w3": [DEPTH, D, FF], "ffn1_w2": [DEPTH, FF, D],
                  "ffn2_w1": [DEPTH, D, FF], "ffn2_w3": [DEPTH, D, FF], "ffn2_w2": [DEPTH, FF, D]}
        shapes.update({"w_in": [DEPTH, D, DIN], "w_uq": [DEPTH, 256, 768], "w_uq_rot": [DEPTH, 256, 768],
                       "wuk_pad": [DEPTH, 128, 768], "wuv": [DEPTH, 128, 512], "w_pool": [DEPTH, 4, 128, 128],
                       "w_a2": [DEPTH, 2, 16, 256], "b_a2": [DEPTH, 2, 128, 256], "w_branch": [DEPTH, 3, 512, D],
                       "w_out": [DEPTH, D, D], "sel": [2, 32, 96], "cos": [96, L], "sin": [96, L],
                       "band": [4, 5, 128, 128], "tri": [2, 3, 128, 128], "mask": [2, 128, 256]})
        for k, v in shapes.items():
            self.W[k] = B.dram_in(k, v)
        dbg = cfg.get("debug", False)

        def scratch(name, shape, dtype):
            if dbg:
                return DT(nc.dram_tensor(name, list(shape), dtype, kind="ExternalOutput").ap(), name)
            return B.dram_tmp(name, shape, dtype)
        self.uT_d = scratch("uT_d", [D, T], BF16)
        self.qfT = scratch("qfT", [8, 96, T], BF16)
        self.kfT = scratch("kfT", [8, 96, T], BF16)
        self.v_tm = scratch("v_tm", [T, 512], BF16)
        self.pz_tm = scratch("pz_tm", [T, 512], BF16)
        self.gqkT = scratch("gqkT", [512, T], BF16)
        self.gkv_tm = scratch("gkv_tm", [T, 768], BF16)
        self.gaT = scratch("gaT", [2, 16, T], BF16)
        self.grT = scratch("grT", [512, T], BF16)
        self.oaT = scratch("oaT", [512, T], BF16)
        self.opT = scratch("opT", [512, T], BF16)
        self.ogT = scratch("ogT", [512, T], BF16)
        self.ofT = scratch("ofT", [512, T], F32)
        self.out = DT(nc.dram_tensor("out", [L, D], F32, kind="ExternalOutput").ap(), "out")
        self.hT = B.dram_tmp("hT", [D, T], F32)
        self.ident = B.alloc(128)
        self.ones_bf = B.alloc(128, BF16)
        self.ones_f = B.alloc(64)
        self.vecs = B.alloc(DEPTH * NVL)
        self.bada = B.alloc(DEPTH * 72)
        self.modv = B.alloc(DEPTH * 2 * 72)
        self.dv = B.alloc(DEPTH * 2 * 72)
        B.persist()
        B.dma("sp", self.ident, self.idn_in.t(0))
        B.dma("sp", self.vecs, self.vecs_in.t(0))
        B.dma("sp", self.bada, self.bada_in.t(0))
        B.memset("dve", self.ones_bf, 1.0)
        B.memset("dve", self.ones_f, 1.0)
        self.finals = []

    def vec(self, l, col, n=1, rows=128):
        return self.vecs[0:rows, l * NVL + col: l * NVL + col + n]

    def mod(self, l, s, m):
        o = (l * 2 + s) * 72 + m * 8
        return self.modv[:, o:o + 8]

    def dvv(self, l, s, m):
        o = (l * 2 + s) * 72 + m * 8
        return self.dv[:, o:o + 8]

    def stage_xin(self):
        B = self.B
        B.new_stage()
        xs = [B.alloc(4 * D) for _ in range(2)]
        hs = [B.alloc(8 * 512) for _ in range(2)]
        hTv = self.hT.ap.rearrange("(c p) t -> p c t", p=128)
        for ti, (t0, n) in enumerate(TILES):
            if self.cfg.get("quick") and ti > 1:
                break
            nb = n // 128
            x_t = xs[ti % 2].re("p (b d) -> p b d", b=4)
            h_t = hs[ti % 2].re("p (c t) -> p c t", c=8)
            if ti == 0:
                src = self.ctx_in.t(0, self.ctx_in.ap.rearrange("(b p) d -> p b d", p=128))
            else:
                src = self.x_in.t(ti, self.x_in.ap[t0 - LC:t0 - LC + n, :].rearrange("(b p) d -> p b d", p=128))
            B.dma("sp", x_t[:, 0:nb, :], src)
            for c in range(8):
                ps = B.psum[c]
                for b in range(nb):
                    B.tr(ps[:, b * 128:(b + 1) * 128], x_t[:, b, c * 128:(c + 1) * 128], self.ident)
                B.copy("dve" if c % 2 == 0 else "act", h_t[:, c, 0:n], ps[:, 0:n])
            B.dma("sp", self.hT.t(ti, hTv[:, :, t0:t0 + n]), h_t[:, :, 0:n])

    def stage_xout(self):
        B = self.B
        B.new_stage()
        hs = [B.alloc(8 * 512) for _ in range(2)]
        os_ = [B.alloc(4 * D) for _ in range(2)]
        hTv = self.hT.ap.rearrange("(c p) t -> p c t", p=128)
        k = 0
        for ti, (t0, n) in enumerate(TILES):
            if ti == 0 or (self.cfg.get("quick") and ti > 1):
                continue
            h_t = hs[ti % 2].re("p (c t) -> p c t", c=8)
            o_t = os_[ti % 2].re("p (b d) -> p b d", b=4)
            B.dma("sp", h_t, self.hT.t(ti, hTv[:, :, t0:t0 + n]))
            for b in range(4):
                for half in range(2):
                    ps = B.psum[k % 8]
                    k += 1
                    for cc in range(4):
                        c = half * 4 + cc
                        B.tr(ps[:, cc * 128:(cc + 1) * 128], h_t[:, c, b * 128:(b + 1) * 128], self.ident)
                    B.copy("dve" if k % 2 == 0 else "act", o_t[:, b, half * 512:(half + 1) * 512], ps)
            dst = self.out.t(ti, self.out.ap[t0 - LC:t0 - LC + n, :].rearrange("(b p) d -> p b d", p=128))
            self.finals.append(B.dma("sp", dst, o_t))

    def stage_adaln(self):
        B = self.B
        B.new_stage()
        cv = B.alloc(16)
        sc = B.alloc(16)
        B.dma("sp", cv, self.cvec.t(0))
        B.act(sc, cv, AF.Silu)
        sc3 = sc.re("p (c s) -> p c s", s=2)
        wb = [B.alloc(8 * 512) for _ in range(3)]
        k = 0
        for l in range(DEPTH):
            mps = B.psum[l][:, 0:144]
            for j in range(18):
                w_t = wb[k % 3].re("p (c n) -> p c n", c=8)
                k += 1
                src = self.W["w_ada"].t(0, self.W["w_ada"].ap[l, :, j * 512:(j + 1) * 512].rearrange("(c p) n -> p c n", p=128))
                B.dma("sp" if k % 2 == 0 else "act", w_t, src)
                for i in range(4):
                    col = (j * 4 + i) * 2
                    for c in range(8):
                        B.mm(mps[:, col:col + 2], w_t[:, c, i * 128:(i + 1) * 128], sc3[:, c, :], start=(c == 0), stop=(c == 7))
            m3 = mps.re("p (i s) -> p i s", s=2)
            for s in range(2):
                o = (l * 2 + s) * 72
                B.tt("dve", self.modv[:, o:o + 72], m3[:, :, s], self.bada[:, l * 72:(l + 1) * 72], ALU.add)
            for s in range(2):
                for (m_scale, gcol, m_out) in ((1, 0, 1), (4, 8, 4), (7, 16, 7)):
                    B.stt("dve", self.dvv(l, s, m_out), self.mod(l, s, m_scale), 1.0, self.vec(l, gcol, 8), ALU.add, ALU.mult)
                for m in (2, 8):
                    B.ts("dve", self.dvv(l, s, m), self.mod(l, s, m), 0.5, None, ALU.mult)

    def load_cast(self, dst, src_ap, src_dt, stg, k):
        B = self.B
        s = stg[k % len(stg)]
        n = dst.ap.shape[-1] if len(dst.ap.shape) == 2 else None
        B.dma("sp" if k % 2 == 0 else "act", s, src_dt.t(0, src_ap))
        B.copy("pool", dst, s)

    def stage_ffn(self, l, which, tiles):
        B = self.B
        B.new_stage()
        pre = "ffn1" if which == 1 else "ffn2"
        m0 = 0 if which == 1 else 6
        w1 = B.alloc(8 * FF, BF16).re("p (c f) -> p c f", c=8)
        w3 = B.alloc(8 * FF, BF16).re("p (c f) -> p c f", c=8)
        w2 = B.alloc(NFF * D, BF16).re("p (f d) -> p f d", f=NFF)
        stg = [B.alloc(1024) for _ in range(2)]
        k = 0
        for c in range(8):
            for (wt, nm) in ((w1, "_w1"), (w3, "_w3")):
                src = self.W[pre + nm]
                for (c0, cn) in ((0, 1024), (1024, 1024), (2048, 768)):
                    B.dma("sp" if k % 2 == 0 else "act", stg[k % 2][:, 0:cn], src.t(0, src.ap[l, c * 128:(c + 1) * 128, c0:c0 + cn]))
                    B.copy("pool", wt[:, c, c0:c0 + cn], stg[k % 2][:, 0:cn])
                    k += 1
        src = self.W[pre + "_w2"]
        for f in range(NFF):
            B.dma("sp" if k % 2 == 0 else "act", stg[k % 2], src.t(0, src.ap[l, f * 128:(f + 1) * 128, :]))
            B.copy("pool", w2[:, f, :], stg[k % 2])
            k += 1
        hs = B.alloc(8 * 512).re("p (c t) -> p c t", c=8)
        uT = B.alloc(8 * 512, BF16).re("p (c t) -> p c t", c=8)
        gT = B.alloc(NFF * 512, BF16).re("p (f t) -> p f t", f=NFF)
        sq = [B.alloc(512, BF16) for _ in range(2)]
        rs = B.alloc(512)
        tmp = [B.alloc(512)]
        sa = [B.alloc(512) for _ in range(2)]
        hTv = self.hT.ap.rearrange("(c p) t -> p c t", p=128)
        ps = B.psum
        ka = 0
        for ti in tiles:
            t0, n = TILES[ti]
            s = 0 if ti > 0 else 1
            gs, sh, hg = self.dvv(l, s, m0 + 1), self.mod(l, s, m0), self.dvv(l, s, m0 + 2)
            B.dma("sp", hs[:, :, 0:n], self.hT.t(ti, hTv[:, :, t0:t0 + n]))
            for c in range(8):
                B.act(sq[c % 2][:, 0:n], hs[:, c, 0:n], AF.Square)
                B.mm(ps[6][:, 0:n], self.ones_bf, sq[c % 2][:, 0:n], start=(c == 0), stop=(c == 7))
            B.act(rs[:, 0:n], ps[6][:, 0:n], AF.Sqrt, bias=EPS, scale=1.0 / D)
            B.S.op("dve", lambda e, o=rs[:, 0:n].ap: e.reciprocal(o, o), [rs.buf], [rs.buf])
            for c in range(8):
                B.stt("dve", tmp[0][:, 0:n], hs[:, c, 0:n], gs[:, c:c + 1], rs[:, 0:n], ALU.mult, ALU.mult)
                B.act(uT[:, c, 0:n], tmp[0][:, 0:n], AF.Identity, bias=sh[:, c:c + 1], scale=1.0)
            for f in range(NFF):
                pa, pb = ps[ka % 2], ps[2 + ka % 2]
                for c in range(8):
                    B.mm(pa[:, 0:n], w1[:, c, f * 128:(f + 1) * 128], uT[:, c, 0:n], start=(c == 0), stop=(c == 7))
                for c in range(8):
                    B.mm(pb[:, 0:n], w3[:, c, f * 128:(f + 1) * 128], uT[:, c, 0:n], start=(c == 0), stop=(c == 7))
                B.act(sa[ka % 2][:, 0:n], pa[:, 0:n], AF.Silu)
                B.tt("dve", gT[:, f, 0:n], sa[ka % 2][:, 0:n], pb[:, 0:n], ALU.mult)
                ka += 1
            for d in range(8):
                po = ps[4 + d % 2]
                for f in range(NFF):
                    B.mm(po[:, 0:n], w2[:, f, d * 128:(d + 1) * 128], gT[:, f, 0:n], start=(f == 0), stop=(f == NFF - 1))
                B.stt("dve", hs[:, d, 0:n], po[:, 0:n], hg[:, d:d + 1], hs[:, d, 0:n], ALU.mult, ALU.add)
            B.dma("sp", self.hT.t(ti, hTv[:, :, t0:t0 + n]), hs[:, :, 0:n])

    def wload(self, dst, src, ap):
        B = self.B
        k = self.wk
        self.wk += 1
        P, n = ap.shape[0], ap.shape[-1]
        st = self.stg[k % 2][0:P, 0:n]
        B.dma("sp" if k % 2 == 0 else "act", st, src.t(0, ap))
        B.copy("pool", dst, st)

    def norm_u(self, l, s, hs, uT, n, gs, sh, sq, rs, tmp):
        B = self.B
        ps = B.psum
        for c in range(8):
            B.act(sq[c % 2][:, 0:n], hs[:, c, 0:n], AF.Square)
            B.mm(ps[6][:, 0:n], self.ones_bf, sq[c % 2][:, 0:n], start=(c == 0), stop=(c == 7))
        B.act(rs[:, 0:n], ps[6][:, 0:n], AF.Sqrt, bias=EPS, scale=1.0 / D)
        B.S.op("dve", lambda e, o=rs[:, 0:n].ap: e.reciprocal(o, o), [rs.buf], [rs.buf])
        for c in range(8):
            B.stt("dve", tmp[:, 0:n], hs[:, c, 0:n], gs[:, c:c + 1], rs[:, 0:n], ALU.mult, ALU.mult)
            B.act(uT[:, c, 0:n], tmp[:, 0:n], AF.Identity, bias=sh[:, c:c + 1], scale=1.0)

    def stage_in(self, l):
        B = self.B
        B.new_stage()
        ps = B.psum
        self.stg = [B.alloc(1024) for _ in range(2)]
        self.wk = 0
        NA = 2496
        win = B.alloc(8 * NA, BF16).re("p (c f) -> p c f", c=8)
        wuq = B.alloc(2 * 768, BF16).re("p (c f) -> p c f", c=2)
        wuqr = B.alloc(2 * 768, BF16).re("p (c f) -> p c f", c=2)
        wuk = B.alloc(768, BF16)
        wuv = B.alloc(512, BF16)
        sel = B.alloc(192, BF16)
        W = self.W
        for c in range(8):
            for (c0, cn) in ((0, 1024), (1024, 1024), (2048, NA - 2048)):
                self.wload(win[:, c, c0:c0 + cn], W["w_in"], W["w_in"].ap[l, c * 128:(c + 1) * 128, c0:c0 + cn])
        for c in range(2):
            self.wload(wuq[:, c, :], W["w_uq"], W["w_uq"].ap[l, c * 128:(c + 1) * 128, :])
            self.wload(wuqr[:, c, :], W["w_uq_rot"], W["w_uq_rot"].ap[l, c * 128:(c + 1) * 128, :])
        self.wload(wuk, W["wuk_pad"], W["wuk_pad"].ap[l])
        self.wload(wuv, W["wuv"], W["wuv"].ap[l])
        for i in range(2):
            self.wload(sel[0:32, i * 96:(i + 1) * 96], W["sel"], W["sel"].ap[i])
        hs = B.alloc(8 * 512).re("p (c t) -> p c t", c=8)
        uT = B.alloc(8 * 512, BF16).re("p (c t) -> p c t", c=8)
        sq = [B.alloc(512, BF16) for _ in range(2)]
        rs = B.alloc(512)
        tmp = B.alloc(512)
        cq_sb = B.alloc(2 * 512).re("p (c t) -> p c t", c=2)
        cqn = B.alloc(2 * 512, BF16).re("p (c t) -> p c t", c=2)
        ckv_sb = B.alloc(512)
        ckvn = B.alloc(512, BF16)
        kr_bf = B.alloc(512, BF16)
        krrot = B.alloc(512)
        cos_t = B.alloc(512)
        sin_t = B.alloc(512)
        gr_sb = B.alloc(4 * 512, BF16).re("p (c t) -> p c t", c=4)
        gqk_sb = B.alloc(4 * 512, BF16).re("p (c t) -> p c t", c=4)
        ga_sb = B.alloc(2 * 512, BF16).re("p (c t) -> p c t", c=2)
        qf_sb = B.alloc(8 * 512, BF16).re("p (c t) -> p c t", c=8)
        kf_sb = B.alloc(8 * 512, BF16).re("p (c t) -> p c t", c=8)
        v_sb = B.alloc(4 * 512, BF16).re("p (b f) -> p b f", b=4)
        pz_sb = B.alloc(4 * 512, BF16).re("p (b f) -> p b f", b=4)
        gkv_sb = B.alloc(4 * 768, BF16).re("p (b f) -> p b f", b=4)
        hsq = [B.alloc(512, BF16) for _ in range(2)]
        hrs = [B.alloc(512) for _ in range(2)]
        hn = [B.alloc(512) for _ in range(2)]
        hrn = [B.alloc(512) for _ in range(2)]
        ht1 = [B.alloc(512) for _ in range(2)]
        ht2 = [B.alloc(512) for _ in range(2)]
        hTv = self.hT.ap.rearrange("(c p) t -> p c t", p=128)
        kb = 0
        kt = 0
        kh = 0
        for ti in range(2 if self.cfg.get("quick") else 9):
            t0, n = TILES[ti]
            nb = n // 128
            s = 0 if ti > 0 else 1
            B.dma("sp", hs[:, :, 0:n], self.hT.t(ti, hTv[:, :, t0:t0 + n]))
            if ti > 0 and not self.cfg.get("no_cos"):
                B.dma("sp", cos_t[0:96, 0:n], W["cos"].t(0, W["cos"].ap[:, t0 - LC:t0 - LC + n]))
                B.dma("sp", sin_t[0:96, 0:n], W["sin"].t(0, W["sin"].ap[:, t0 - LC:t0 - LC + n]))
            self.norm_u(l, s, hs, uT, n, self.dvv(l, s, 4), self.mod(l, s, 3), sq, rs, tmp)
            if not self.cfg.get("no_ut"):
                B.dma("sp", self.uT_d.t(ti, self.uT_d.ap.rearrange("(c p) t -> p c t", p=128)[:, :, t0:t0 + n]), uT[:, :, 0:n])

            stop = self.cfg.get("in_stop", 9)
            if stop <= 1:
                continue

            def fm(col0, m):
                nonlocal kb
                p = ps[kb % 4]
                kb += 1
                for c in range(8):
                    B.mm(p[0:m, 0:n], win[:, c, col0:col0 + m], uT[:, c, 0:n], start=(c == 0), stop=(c == 7))
                return p
            sub2 = self.cfg.get("sub2", 9)
            for j in range(2):
                p = fm(j * 128, 128)
                B.copy("dve", cq_sb[:, j, 0:n], p[:, 0:n])
                if sub2 >= 2:
                    B.act(sq[j][:, 0:n], cq_sb[:, j, 0:n], AF.Square)
            if sub2 >= 3:
                for j in range(2):
                    B.mm(ps[6][:, 0:n], self.ones_bf, sq[j][:, 0:n], start=(j == 0), stop=(j == 1))
                B.act(rs[:, 0:n], ps[6][:, 0:n], AF.Sqrt, bias=EPS, scale=1.0 / 256)
                B.S.op("dve", lambda e, o=rs[:, 0:n].ap: e.reciprocal(o, o), [rs.buf], [rs.buf])
            if sub2 >= 4:
                for j in range(2):
                    B.stt("dve", cqn[:, j, 0:n], cq_sb[:, j, 0:n], self.vec(l, 28 + j), rs[:, 0:n], ALU.mult, ALU.mult)
            if sub2 < 9:
                continue
            p = fm(256, 128)
            B.copy("dve", ckv_sb[:, 0:n], p[:, 0:n])
            B.act(sq[0][:, 0:n], ckv_sb[:, 0:n], AF.Square)
            B.mm(ps[6][:, 0:n], self.ones_bf, sq[0][:, 0:n])
            B.act(rs[:, 0:n], ps[6][:, 0:n], AF.Sqrt, bias=EPS, scale=1.0 / 128)
            B.S.op("dve", lambda e, o=rs[:, 0:n].ap: e.reciprocal(o, o), [rs.buf], [rs.buf])
            B.stt("dve", ckvn[:, 0:n], ckv_sb[:, 0:n], self.vec(l, 30), rs[:, 0:n], ALU.mult, ALU.mult)
            if self.cfg.get("sub", 9) <= 1:
                continue
            p = fm(384, 32)
            B.copy("act", kr_bf[0:32, 0:n], p[0:32, 0:n])
            if ti > 0 and self.cfg.get("sub", 9) > 2:
                p = ps[kb % 4]
                kb += 1
                B.mm(p[0:96, 0:n], sel[0:32, 96:192], kr_bf[0:32, 0:n])
                B.copy("act", krrot[0:96, 0:n], p[0:96, 0:n])
            if stop <= 2:
                continue
            for h in range(8):
                for isk in range(2):
                    i2 = kh % 2
                    kh += 1
                    p = ps[kb % 4]
                    kb += 1
                    if isk == 0:
                        for j in range(2):
                            B.mm(p[0:96, 0:n], wuq[:, j, h * 96:(h + 1) * 96], cqn[:, j, 0:n], start=(j == 0), stop=(j == 1))
                        gcol, rcol, dst = 31, 32, qf_sb
                    else:
                        B.mm(p[0:96, 0:n], wuk[:, h * 96:(h + 1) * 96], ckvn[:, 0:n], start=True, stop=False)
                        B.mm(p[0:96, 0:n], sel[0:32, 0:96], kr_bf[0:32, 0:n], start=False, stop=True)
                        gcol, rcol, dst = 33, 34, kf_sb
                    B.copy("act", ht1[i2][0:96, 0:n], p[0:96, 0:n])
                    p = ht1[i2]
                    B.act(hsq[i2][0:96, 0:n], p[0:96, 0:n], AF.Square)
                    B.mm(ps[7][0:96, 0:n], self.ones_bf[0:96, 0:96], hsq[i2][0:96, 0:n])
                    B.act(hrs[i2][0:96, 0:n], ps[7][0:96, 0:n], AF.Sqrt, bias=EPS, scale=1.0 / 96)
                    B.S.op("dve", lambda e, o=hrs[i2][0:96, 0:n].ap: e.reciprocal(o, o), [hrs[i2].buf], [hrs[i2].buf])
                    if ti == 0:
                        B.stt("dve", dst[0:96, h, 0:n], p[0:96, 0:n], self.vec(l, gcol, 1, 96), hrs[i2][0:96, 0:n], ALU.mult, ALU.mult)
                        continue
                    B.stt("dve", hn[i2][0:96, 0:n], p[0:96, 0:n], self.vec(l, gcol, 1, 96), hrs[i2][0:96, 0:n], ALU.mult, ALU.mult)
                    if isk == 0:
                        p2 = ps[kb % 4]
                        kb += 1
                        for j in range(2):
                            B.mm(p2[0:96, 0:n], wuqr[:, j, h * 96:(h + 1) * 96], cqn[:, j, 0:n], start=(j == 0), stop=(j == 1))
                        rsrc = p2[0:96, 0:n]
                    else:
                        rsrc = krrot[0:96, 0:n]
                    B.stt("dve", hrn[i2][0:96, 0:n], rsrc, self.vec(l, rcol, 1, 96), hrs[i2][0:96, 0:n], ALU.mult, ALU.mult)
                    B.tt("dve", hn[i2][0:96, 0:n], hn[i2][0:96, 0:n], cos_t[0:96, 0:n], ALU.mult)
                    B.tt("dve", ht2[i2][0:96, 0:n], hrn[i2][0:96, 0:n], sin_t[0:96, 0:n], ALU.mult)
                    B.tt("dve", dst[0:96, h, 0:n], hn[i2][0:96, 0:n], ht2[i2][0:96, 0:n], ALU.add)
            B.dma("sp", self.qfT.t(ti, self.qfT.ap[:, :, t0:t0 + n].rearrange("h r t -> r h t")), qf_sb[0:96, :, 0:n])
            B.dma("sp", self.kfT.t(ti, self.kfT.ap[:, :, t0:t0 + n].rearrange("h r t -> r h t")), kf_sb[0:96, :, 0:n])
            if stop <= 3:
                continue
            for j in range(4):
                p = fm(928 + j * 128, 128)
                B.copy("act" if j % 2 else "dve", gqk_sb[:, j, 0:n], p[:, 0:n])
            B.dma("sp", self.gqkT.t(ti, self.gqkT.ap.rearrange("(c p) t -> p c t", p=128)[:, :, t0:t0 + n]), gqk_sb[:, :, 0:n])
            for j in range(2):
                p = fm(1952 + j * 16, 16)
                B.copy("act", ga_sb[0:16, j, 0:n], p[0:16, 0:n])
            B.dma("sp", self.gaT.t(ti, self.gaT.ap[:, :, t0:t0 + n].rearrange("d r t -> r d t")), ga_sb[0:16, :, 0:n])
            for j in range(4):
                p = fm(1984 + j * 128, 128)
                B.act(gr_sb[:, j, 0:n], p[:, 0:n], AF.Silu)
            B.dma("sp", self.grT.t(ti, self.grT.ap.rearrange("(c p) t -> p c t", p=128)[:, :, t0:t0 + n]), gr_sb[:, :, 0:n])
            if stop <= 4:
                continue
            for b in range(nb):
                bs = slice(b * 128, (b + 1) * 128)
                p = ps[4 + kt % 2]
                kt += 1
                B.mm(p[:, 0:512], ckvn[:, bs], wuv)
                B.copy("act", v_sb[:, b, :], p[:, 0:512])
                p = ps[4 + kt % 2]
                kt += 1
                for c in range(8):
                    B.mm(p[:, 0:512], uT[:, c, bs], win[:, c, 416:928], start=(c == 0), stop=(c == 7))
                B.copy("dve", pz_sb[:, b, :], p[:, 0:512])
                p = ps[4 + kt % 2]
                kt += 1
                for c in range(8):
                    B.mm(p[:, 0:256], uT[:, c, bs], win[:, c, 1184:1440], start=(c == 0), stop=(c == 7))
                B.copy("act", gkv_sb[:, b, 0:256], p[:, 0:256])
                p = ps[4 + kt % 2]
                kt += 1
                for c in range(8):
                    B.mm(p[:, 0:512], uT[:, c, bs], win[:, c, 1440:1952], start=(c == 0), stop=(c == 7))
                B.copy("dve", gkv_sb[:, b, 256:768], p[:, 0:512])
            B.dma("sp", self.v_tm.t(ti, self.v_tm.ap[t0:t0 + n, :].rearrange("(b p) f -> p b f", p=128)), v_sb[:, 0:nb, :])
            B.dma("sp", self.pz_tm.t(ti, self.pz_tm.ap[t0:t0 + n, :].rearrange("(b p) f -> p b f", p=128)), pz_sb[:, 0:nb, :])
            B.dma("sp", self.gkv_tm.t(ti, self.gkv_tm.ap[t0:t0 + n, :].rearrange("(b p) f -> p b f", p=128)), gkv_sb[:, 0:nb, :])

    def stage_att(self, l):
        B = self.B
        B.new_stage()
        ps = B.psum
        kf = [B.alloc(T, BF16) for _ in range(2)]
        qf = [B.alloc(T, BF16) for _ in range(2)]
        v_all = B.alloc(34 * 512, BF16).re("p (c f) -> p c f", c=34)
        vx = B.alloc(34 * 8 * 72, BF16).re("p (c h d) -> p c h d", c=34, h=8)
        pT = [B.alloc(512, BF16) for _ in range(3)]
        o_sb = [B.alloc(512) for _ in range(2)]
        oa_sb = [B.alloc(512, BF16) for _ in range(2)]
        B.memset("pool", vx, 1.0)
        vsrc = self.v_tm.ap.rearrange("(c p) f -> p c f", p=128)
        for c0 in range(0, 34, 9):
            c1 = min(c0 + 9, 34)
            B.dma("sp", v_all[:, c0:c1, :], self.v_tm.t("all", vsrc[:, c0:c1, :]))
        B.copy("pool", vx[:, :, :, 0:64], v_all.re("p c (h d) -> p c h d", h=8))
        qtiles = list(range(1, 9)) + ([0] if l < DEPTH - 1 else [])
        if self.cfg.get("quick"):
            qtiles = [1]
        kp = 0
        ko = 0
        sc = 96 ** -0.5
        for h in range(1 if self.cfg.get("quick") else 8):
            i2 = h % 2
            B.dma("sp", kf[i2][0:96, :], self.kfT.t("all", self.kfT.ap[h]))
            B.dma("sp", qf[i2][0:96, :], self.qfT.t("all", self.qfT.ap[h]))
            for ti in qtiles:
                t0, n = TILES[ti]
                nkc = 34 if ti > 0 else 2
                po = ps[6 + ko % 2]
                for kc in range(nkc):
                    sp_ = ps[kp % 4]
                    B.mm(sp_[:, 0:n], kf[i2][0:96, kc * 128:(kc + 1) * 128], qf[i2][0:96, t0:t0 + n])
                    B.act(pT[kp % 3][:, 0:n], sp_[:, 0:n], AF.Exp, scale=sc)
                    B.mm(po[0:66, 0:n], vx[:, kc, h, 0:66], pT[kp % 3][:, 0:n], start=(kc == 0), stop=(kc == nkc - 1))
                    kp += 1
                osb = o_sb[ko % 2]
                B.copy("dve", osb[0:65, 0:n], po[0:65, 0:n])
                B.S.op("dve", lambda e, o=osb[64:65, 0:n].ap: e.reciprocal(o, o), [osb.buf], [osb.buf])
                pb = ps[4 + ko % 2]
                B.mm(pb[0:64, 0:n], self.ones_f[64:65, 0:64], osb[64:65, 0:n])
                B.tt("dve", oa_sb[ko % 2][0:64, 0:n], osb[0:64, 0:n], pb[0:64, 0:n], ALU.mult)
                B.dma("sp", self.oaT.t((h, ti), self.oaT.ap[h * 64:(h + 1) * 64, t0:t0 + n]), oa_sb[ko % 2][0:64, 0:n])
                ko += 1

    def stage_pool(self, l):
        B = self.B
        B.new_stage()
        ps = B.psum
        W = self.W
        self.stg = [B.alloc(1024) for _ in range(2)]
        self.wk = 0
        pz = B.alloc(34 * 512, BF16).re("p (b f) -> p b f", b=34)
        band = B.alloc(20 * 128, BF16).re("p (g k t) -> p g k t", g=4, k=5)
        wp = B.alloc(4 * 128, BF16).re("p (g d) -> p g d", g=4)
        for g in range(4):
            for k in range(5):
                self.wload(band[:, g, k, :], W["band"], W["band"].ap[g, k])
            self.wload(wp[:, g, :], W["w_pool"], W["w_pool"].ap[l, g])
        psrc = self.pz_tm.ap.rearrange("(b p) f -> p b f", p=128)
        for c0 in range(0, 34, 9):
            c1 = min(c0 + 9, 34)
            B.dma("sp", pz[:, c0:c1, :], self.pz_tm.t("all", psrc[:, c0:c1, :]))
        pooled = [B.alloc(512, BF16) for _ in range(2)]
        op_sb = [B.alloc(4 * 512, BF16).re("p (g t) -> p g t", g=4) for _ in range(2)]
        tiles = list(range(1, 9)) + ([0] if l < DEPTH - 1 else [])
        if self.cfg.get("quick"):
            tiles = [1]
        kk = 0
        for ti in tiles:
            t0, n = TILES[ti]
            nb = n // 128
            b0 = t0 // 128
            first, last = (0, 1) if ti == 0 else (2, 33)
            osb = op_sb[kk % 2]
            for g in range(4):
                pp = ps[kk % 2 * 2]
                py = ps[kk % 2 * 2 + 1]
                gs = slice(g * 128, (g + 1) * 128)
                for bi in range(nb):
                    b = b0 + bi
                    terms = []
                    if b > first:
                        terms.append((b - 1, 0))
                    terms.append((b, 3 if b == first else (4 if b == last else 1)))
                    if b < last:
                        terms.append((b + 1, 2))
                    for i, (j, kind) in enumerate(terms):
                        B.mm(pp[:, bi * 128:(bi + 1) * 128], pz[:, j, gs], band[:, g, kind, :], start=(i == 0), stop=(i == len(terms) - 1))
                B.copy("act", pooled[g % 2][:, 0:n], pp[:, 0:n])
                B.mm(py[:, 0:n], wp[:, g, :], pooled[g % 2][:, 0:n])
                B.act(osb[:, g, 0:n], py[:, 0:n], AF.Identity, scale=self.vec(l, 24 + g))
            kk += 1
            B.dma("sp", self.opT.t(ti, self.opT.ap.rearrange("(g p) t -> p g t", p=128)[:, :, t0:t0 + n]), osb[:, :, 0:n])

    def stage_gla(self, l):
        B = self.B
        B.new_stage()
        ps = B.psum
        W = self.W
        self.stg = [B.alloc(1024) for _ in range(2)]
        self.wk = 0
        wa2 = B.alloc(2 * 256, BF16).re("p (d f) -> p d f", d=2)
        tri = B.alloc(6 * 128).re("p (d k t) -> p d k t", d=2, k=3)
        mask = B.alloc(2 * 256).re("p (d f) -> p d f", d=2)
        bias = B.alloc(2 * 256).re("p (d f) -> p d f", d=2)
        for d in range(2):
            self.wload(wa2[0:16, d, :], W["w_a2"], W["w_a2"].ap[l, d])
            for k in range(2):
                B.dma("sp", tri[:, d, k, :], W["tri"].t(0, W["tri"].ap[d, k]))
            B.dma("sp", mask[:, d, :], W["mask"].t(0, W["mask"].ap[d]))
            B.dma("sp", bias[:, d, :], W["b_a2"].t(0, W["b_a2"].ap[l, d]))
        S32 = B.alloc(512).re("p (h v) -> p h v", h=4)
        S16 = [B.alloc(512, BF16).re("p (h v) -> p h v", h=4) for _ in range(2)]
        qk = [B.alloc(8 * 128, BF16).re("p (w h t) -> p w h t", w=2, h=4) for _ in range(2)]
        gkv = [B.alloc(2 * 768, BF16).re("p (c f) -> p c f", c=2) for _ in range(2)]
        ga = [B.alloc(128, BF16) for _ in range(2)]
        gr = [B.alloc(4 * 128, BF16).re("p (c t) -> p c t", c=4) for _ in range(2)]
        of = [B.alloc(4 * 128).re("p (c t) -> p c t", c=4) for _ in range(2)]
        xb = B.alloc(256)
        xe = B.alloc(256)
        nla = B.alloc(256)
        eq = B.alloc(512).re("p (h t) -> p h t", h=4)
        ek = B.alloc(512).re("p (h t) -> p h t", h=4)
        qt = B.alloc(512, BF16).re("p (h t) -> p h t", h=4)
        ktt = B.alloc(512, BF16).re("p (h t) -> p h t", h=4)
        er = B.alloc(512).re("p (c f) -> p c f", c=2)
        kend = B.alloc(512, BF16).re("p (c f) -> p c f", c=2)
        at_sb = B.alloc(512, BF16)
        osum = B.alloc(512)
        osq = B.alloc(512, BF16)
        ors = B.alloc(512)
        og1 = B.alloc(512)
        og_sb = [B.alloc(512, BF16).re("p (c t) -> p c t", c=4) for _ in range(2)]
        need_ctx_out = l < DEPTH - 1
        qksrc = self.gqkT.ap.rearrange("(w h k) t -> k w h t", w=2, h=4)
        for d in range(2):
            B.memset("dve", S32[0:64], 0.0)
            B.memset("dve", S16[0][0:64], 0.0)
            spar = 0
            blocks = list(range(34)) if d == 0 else ([1, 0] + list(range(33, 1, -1)))
            if self.cfg.get("quick"):
                blocks = blocks[0:4]
            for bi, b in enumerate(blocks):
                i2 = bi % 2
                t0 = b * 128
                B.dma("sp", qk[i2][0:64], self.gqkT.t("all", qksrc[:, :, :, t0:t0 + 128]))
                B.dma("sp", gkv[i2][0:64], self.gkv_tm.t("all", self.gkv_tm.ap[t0:t0 + 128, :].rearrange("(c p) f -> p c f", p=64)))
                B.dma("sp", ga[i2][0:16, :], self.gaT.t("all", self.gaT.ap[d, :, t0:t0 + 128]))
                want_out = (b >= 2) or need_ctx_out
                if d == 1 and want_out:
                    B.dma("sp", gr[i2], self.grT.t("all", self.grT.ap.rearrange("(c p) t -> p c t", p=128)[:, :, t0:t0 + 128]))
                    B.dma("sp", of[i2], self.ofT.t(b, self.ofT.ap.rearrange("(c p) t -> p c t", p=128)[:, :, t0:t0 + 128]))
                B.mm(ps[0][:, 0:256], ga[i2][0:16, :], wa2[0:16, d, :])
                B.tt("dve", xb, ps[0][:, 0:256], bias[:, d, :], ALU.add)
                B.act(xe, xb, AF.Exp, scale=-1.0)
                B.act(nla, xe, AF.Ln, bias=1.0, scale=1.0)
                for h in range(4):
                    B.mm(ps[1][0:64, h * 128:(h + 1) * 128], nla[:, h * 64:(h + 1) * 64], tri[:, d, 0, :])
                for cc in range(2):
                    B.mm(ps[2][0:64, cc * 256:(cc + 1) * 256], tri[:, d, 1, cc * 64:(cc + 1) * 64], nla)
                p1 = ps[1].re("p (h t) -> p h t", h=4)
                B.act(eq[0:64], p1[0:64], AF.Exp, scale=-1.0 / 16)
                B.act(ek[0:64], p1[0:64], AF.Exp, scale=1.0 / 16)
                B.stt("dve", qt[0:64], qk[i2][0:64, 0], 0.125, eq[0:64], ALU.mult, ALU.mult)
                B.tt("dve", ktt[0:64], qk[i2][0:64, 1], ek[0:64], ALU.mult)
                B.act(er[0:64], ps[2].re("p (c f) -> p c f", c=2)[0:64], AF.Exp, scale=-1.0 / 16)
                B.tt("dve", kend[0:64], gkv[i2][0:64, :, 0:256], er[0:64], ALU.mult)
                for cc in range(2):
                    cs = slice(cc * 64, (cc + 1) * 64)
                    for h in range(4):
                        o0 = (cc * 4 + h) * 64
                        B.mm(ps[3][0:64, o0:o0 + 64], ktt[0:64, h, cs], qt[0:64, h, cs])
                for cc in range(2):
                    B.tt("dve", at_sb[0:64, cc * 256:(cc + 1) * 256], ps[3][0:64, cc * 256:(cc + 1) * 256], mask[0:64, d, :], ALU.mult)
                po = ps[4 + bi % 2]
                for cc in ([0, 1] if d == 0 else [1, 0]):
                    cs = slice(cc * 64, (cc + 1) * 64)
                    endcol = cc * 64 + (63 if d == 0 else 0)
                    for h in range(4):
                        oc = slice(h * 128 + cc * 64, h * 128 + (cc + 1) * 64)
                        vv = gkv[i2][0:64, cc, 256 + h * 128:256 + (h + 1) * 128]
                        if want_out:
                            o0 = (cc * 4 + h) * 64
                            B.mm(po[:, oc], vv, at_sb[0:64, o0:o0 + 64], start=True, stop=False)
                            B.mm(po[:, oc], S16[spar][0:64, h, :], qt[0:64, h, cs], start=False, stop=True)
                        B.mm(ps[6][0:64, h * 128:(h + 1) * 128], kend[0:64, cc, h * 64:(h + 1) * 64], vv)
                    p6 = ps[6].re("p (h v) -> p h v", h=4)
                    for h in range(4):
                        B.stt("dve", S32[0:64, h, :], S32[0:64, h, :], eq[0:64, h, endcol:endcol + 1], p6[0:64, h, :], ALU.mult, ALU.add)
                    spar ^= 1
                    B.copy("act", S16[spar][0:64], S32[0:64])
                if not want_out:
                    continue
                po4 = po.re("p (c t) -> p c t", c=4)
                if d == 0:
                    B.copy("dve", of[i2], po4)
                    B.dma("sp", self.ofT.t(b, self.ofT.ap.rearrange("(c p) t -> p c t", p=128)[:, :, t0:t0 + 128]), of[i2])
                else:
                    B.tt("dve", osum.re("p (c t) -> p c t", c=4), po4, of[i2], ALU.add)
                    B.act(osq, osum, AF.Square)
                    B.mm(ps[0], self.ones_bf, osq)
                    B.act(ors, ps[0], AF.Sqrt, bias=EPS, scale=1.0 / 128)
                    B.S.op("dve", lambda e, o=ors.ap: e.reciprocal(o, o), [ors.buf], [ors.buf])
                    B.stt("dve", og1, osum, self.vec(l, 35), ors, ALU.mult, ALU.mult)
                    B.tt("dve", og_sb[i2], og1.re("p (c t) -> p c t", c=4), gr[i2], ALU.mult)
                    B.dma("sp", self.ogT.t(b, self.ogT.ap.rearrange("(c p) t -> p c t", p=128)[:, :, t0:t0 + 128]), og_sb[i2])

    def stage_merge(self, l):
        B = self.B
        B.new_stage()
        ps = B.psum
        W = self.W
        self.stg = [B.alloc(1024) for _ in range(2)]
        self.wk = 0
        wg = B.alloc(8 * 3072, BF16).re("p (c f) -> p c f", c=8)
        wbr = B.alloc(12 * D, BF16).re("p (r c f) -> p r c f", r=3, c=4)
        wout = B.alloc(8 * D, BF16).re("p (c f) -> p c f", c=8)
        for c in range(8):
            for q3 in range(3):
                self.wload(wg[:, c, q3 * 1024:(q3 + 1) * 1024], W["w_in"], W["w_in"].ap[l, c * 128:(c + 1) * 128, 2496 + q3 * 1024:2496 + (q3 + 1) * 1024])
            self.wload(wout[:, c, :], W["w_out"], W["w_out"].ap[l, c * 128:(c + 1) * 128, :])
        for r in range(3):
            for c in range(4):
                self.wload(wbr[:, r, c, :], W["w_branch"], W["w_branch"].ap[l, r, c * 128:(c + 1) * 128, :])
        hs = B.alloc(8 * 512).re("p (c t) -> p c t", c=8)
        uT = B.alloc(8 * 512, BF16).re("p (c t) -> p c t", c=8)
        ob = [B.alloc(4 * 512, BF16).re("p (c t) -> p c t", c=4) for _ in range(3)]
        mT = B.alloc(8 * 512, BF16).re("p (c t) -> p c t", c=8)
        sg = [B.alloc(512) for _ in range(2)]
        macc = B.alloc(512)
        mtmp = [B.alloc(512) for _ in range(2)]
        hTv = self.hT.ap.rearrange("(c p) t -> p c t", p=128)
        tiles = list(range(1, 9)) + ([0] if l < DEPTH - 1 else [])
        srcs = [self.oaT, self.opT, self.ogT]
        kk = 0
        if self.cfg.get("quick"):
            tiles = [1]
        for ti in tiles:
            t0, n = TILES[ti]
            s = 0 if ti > 0 else 1
            B.dma("sp", hs[:, :, 0:n], self.hT.t(ti, hTv[:, :, t0:t0 + n]))
            B.dma("sp", uT[:, :, 0:n], self.uT_d.t("all", self.uT_d.ap.rearrange("(c p) t -> p c t", p=128)[:, :, t0:t0 + n]))
            for r in range(3):
                B.dma("sp", ob[r][:, :, 0:n], srcs[r].t("all", srcs[r].ap.rearrange("(c p) t -> p c t", p=128)[:, :, t0:t0 + n]))
            for dch in range(8):
                ds_ = slice(dch * 128, (dch + 1) * 128)
                for r in range(3):
                    pg = ps[kk % 2]
                    pp = ps[2 + kk % 2]
                    gc = r * 1024 + dch * 128
                    for c in range(8):
                        B.mm(pg[:, 0:n], wg[:, c, gc:gc + 128], uT[:, c, 0:n], start=(c == 0), stop=(c == 7))
                    for c in range(4):
                        B.mm(pp[:, 0:n], wbr[:, r, c, ds_], ob[r][:, c, 0:n], start=(c == 0), stop=(c == 3))
                    B.act(sg[kk % 2][:, 0:n], pg[:, 0:n], AF.Sigmoid)
                    if r == 0:
                        B.tt("dve", macc[:, 0:n], pp[:, 0:n], sg[kk % 2][:, 0:n], ALU.mult)
                    else:
                        B.tt("dve", mtmp[kk % 2][:, 0:n], pp[:, 0:n], sg[kk % 2][:, 0:n], ALU.mult)
                        if r == 1:
                            B.tt("dve", macc[:, 0:n], macc[:, 0:n], mtmp[kk % 2][:, 0:n], ALU.add)
                        else:
                            B.tt("dve", mT[:, dch, 0:n], macc[:, 0:n], mtmp[kk % 2][:, 0:n], ALU.add)
                    kk += 1
            gm = self.mod(l, s, 5)
            for d2 in range(8):
                py = ps[4 + d2 % 2]
                for dch in range(8):
                    B.mm(py[:, 0:n], wout[:, dch, d2 * 128:(d2 + 1) * 128], mT[:, dch, 0:n], start=(dch == 0), stop=(dch == 7))
                B.stt("dve", hs[:, d2, 0:n], py[:, 0:n], gm[:, d2:d2 + 1], hs[:, d2, 0:n], ALU.mult, ALU.add)
            B.dma("sp", self.hT.t(ti, hTv[:, :, t0:t0 + n]), hs[:, :, 0:n])

    def build(self):
        cfg = self.cfg
        st = cfg["stages"]
        self.stage_xin()
        if "noada" not in st:
            self.stage_adaln()
        for l in range(cfg.get("nlayers", DEPTH)):
            last = l == DEPTH - 1
            if "ffn1" in st:
                self.stage_ffn(l, 1, list(range(9)))
            if "in" in st:
                self.stage_in(l)
            if "att" in st:
                self.stage_att(l)
            if "pool" in st:
                self.stage_pool(l)
            if "gla" in st:
                self.stage_gla(l)
            if "merge" in st:
                self.stage_merge(l)
            if "ffn2" in st:
                self.stage_ffn(l, 2, list(range(1, 9)) + ([] if last else [0]))
        self.stage_xout()
        self.B.S.finish(self.finals)
        return self.nc


def host_inputs(inp, b):
    f = np.float32
    m = {}
    m["x"] = np.ascontiguousarray(inp["x"][b])
    m["ctx"] = np.ascontiguousarray(inp["ctx"][b])
    cv = np.zeros((128, 8, 2), f)
    cv[:, :, 0] = inp["c"][b].reshape(8, 128).T
    cv[:, :, 1] = inp["c_ctx"].reshape(8, 128).T
    m["cvec"] = cv.reshape(128, 16)
    vec = np.zeros((128, DEPTH, NVL), f)
    perm = np.arange(32).reshape(2, 2, 8)[:, ::-1, :].reshape(32)
    for l in range(DEPTH):
        vec[:, l, 0:8] = inp["g_ffn1"][l].reshape(8, 128).T
        vec[:, l, 8:16] = inp["g_mix"][l].reshape(8, 128).T
        vec[:, l, 16:24] = inp["g_ffn2"][l].reshape(8, 128).T
        vec[:, l, 24:28] = inp["pool_scale"][l].reshape(4, 128).T
        vec[:, l, 28:30] = inp["g_cq"][l].reshape(2, 128).T
        vec[:, l, 30] = inp["g_ckv"][l]
        vec[:96, l, 31] = inp["g_qn"][l]
        vec[64:96, l, 32] = inp["g_qn"][l][64:][perm]
        vec[:96, l, 33] = inp["g_kn"][l]
        vec[64:96, l, 34] = inp["g_kn"][l][64:][perm]
        vec[:, l, 35] = inp["g_gla_o"][l]
    m["vecs"] = vec.reshape(128, DEPTH * NVL)
    m["bada"] = np.ascontiguousarray(inp["b_ada"].reshape(DEPTH, 72, 128).transpose(2, 0, 1).reshape(128, DEPTH * 72))
    m["idn"] = np.eye(128, dtype=f)
    for k in ("w_ada", "ffn1_w1", "ffn1_w3", "ffn1_w2", "ffn2_w1", "ffn2_w3", "ffn2_w2", "w_in", "w_uq", "w_pool", "w_a2",
              "w_branch", "w_out"):
        m[k] = inp[k]
    m.update(_consts())
    w_uq, w_ukv = inp["w_uq"], inp["w_ukv"]
    rot = np.zeros((DEPTH, 256, 768), f)
    kpad = np.zeros((DEPTH, 128, 768), f)
    wuv = np.zeros((DEPTH, 128, 512), f)
    for h in range(8):
        rot[:, :, h * 96 + 64:(h + 1) * 96] = w_uq[:, :, h * 96 + 64 + perm]
        kpad[:, :, h * 96:h * 96 + 64] = w_ukv[:, :, h * 128:h * 128 + 64]
        wuv[:, :, h * 64:(h + 1) * 64] = w_ukv[:, :, h * 128 + 64:(h + 1) * 128]
    m["w_uq_rot"], m["wuk_pad"], m["wuv"] = rot, kpad, wuv
    m["b_a2"] = np.ascontiguousarray(np.broadcast_to(inp["b_a2"][:, :, None, :], (DEPTH, 2, 128, 256)))
    return m


_CONSTS = None


def _consts():
    global _CONSTS
    if _CONSTS is not None:
        return _CONSTS
    f = np.float32
    c = {}
    perm = np.arange(32).reshape(2, 2, 8)[:, ::-1, :].reshape(32)
    sel = np.zeros((2, 32, 96), f)
    for r in range(32):
        sel[0, r, 64 + r] = 1.0
        sel[1, perm[r], 64 + r] = 1.0
    c["sel"] = sel
    t = np.arange(L)
    pos = np.stack([(t // 64).astype(f), (t % 64).astype(f)], 0)
    inv = (f(10000.0) ** (-np.arange(0, 16, 2, dtype=f) / f(16))).astype(f)
    ang = (pos[:, None, :] * inv[None, :, None]).astype(f)
    cos = np.ones((96, L), f)
    sin = np.zeros((96, L), f)
    for a in range(2):
        for half in range(2):
            r0 = 64 + a * 16 + half * 8
            cos[r0:r0 + 8] = np.cos(ang[a])
            sin[r0:r0 + 8] = np.sin(ang[a]) * (-1.0 if half == 0 else 1.0)
    c["cos"], c["sin"] = cos, sin
    band = np.zeros((4, 5, 128, 128), f)
    for g, w in enumerate((2, 4, 8, 16)):
        n3 = 384
        A = np.zeros((n3, n3), np.float64)
        for tt_ in range(n3):
            lo, hi = max(tt_ - w // 2, 0), min(tt_ + w // 2, n3)
            A[lo:hi, tt_] = 1.0 / (hi - lo)
            A[tt_, tt_] -= 1.0
        band[g, 0] = A[0:128, 128:256]
        band[g, 1] = A[128:256, 128:256]
        band[g, 2] = A[256:384, 128:256]
        band[g, 3] = A[0:128, 0:128]
        band[g, 4] = A[256:384, 256:384]
    c["band"] = band
    tri = np.zeros((2, 3, 128, 128), f)
    mask = np.zeros((2, 128, 256), f)
    for tp in range(128):
        for t_ in range(128):
            if tp // 64 != t_ // 64:
                continue
            tri[0, 0, tp, t_] = 1.0 if tp <= t_ else 0.0
            tri[1, 0, tp, t_] = 1.0 if tp >= t_ else 0.0
            tri[0, 1, tp, t_] = 1.0 if tp > t_ else 0.0
            tri[1, 1, tp, t_] = 1.0 if tp < t_ else 0.0
    for cc in range(2):
        for j_ in range(64):
            for h in range(4):
                for i_ in range(64):
                    mask[0, cc * 64 + j_, h * 64 + i_] = 1.0 if j_ <= i_ else 0.0
                    mask[1, cc * 64 + j_, h * 64 + i_] = 1.0 if j_ >= i_ else 0.0
    c["tri"], c["mask"] = tri, mask
    _CONSTS = c
    return c


_CFG = {"stages": ["ffn1", "in", "att", "pool", "gla", "merge", "ffn2"], "nlayers": DEPTH}


def run_cores(inp, batches, cfg):
    mdl = Model(cfg)
    nc = mdl.build()
    in_maps = [host_inputs(inp, b) for b in batches]
    res = run_bass_kernel_spmd(nc, in_maps, core_ids=list(range(len(batches))))
    if cfg.get("debug"):
        return res.results
    return np.stack([r["out"] for r in res.results], axis=0)


def kernel(**inputs):
    inp = {k: np.asarray(v) for k, v in inputs.items()}
    return run_cores(inp, list(range(8)), _CFG)
```

```python
import numpy as np
import concourse.bass as bass
import concourse.mybir as mybir
from concourse.bass_utils import run_bass_kernel_spmd

F32 = mybir.dt.float32
BF16 = mybir.dt.bfloat16
AF = mybir.ActivationFunctionType
ALU = mybir.AluOpType


class Buf:
    __slots__ = ("name", "w", "r")

    def __init__(self, name):
        self.name = name
        self.w = None
        self.r = {}


class Op:
    __slots__ = ("eng", "fn", "chan", "idx", "waits", "clock", "sig", "dma")


class Sched:
    NDMA = 8

    def __init__(self, nc):
        self.nc = nc
        self.engs = {"pe": nc.tensor, "act": nc.scalar, "dve": nc.vector, "pool": nc.gpsimd, "sp": nc.sync}
        self.ops = {e: [] for e in self.engs}
        self.cnt = {}
        self.known = {e: {} for e in self.engs}
        self.dma_rr = {e: 0 for e in self.engs}
        self.dma_last = {}
        self.all_ops = []
        self.last = {}
        self.pending = {e: [] for e in self.engs}

    def barrier(self):
        lasts = list(self.last.values())
        for e in self.engs:
            self.pending[e] = list(lasts)

    def op(self, eng, fn, reads=(), writes=(), dma=False):
        o = Op()
        o.eng, o.fn, o.dma, o.sig = eng, fn, dma, dma
        k = self.known[eng]
        need = {}
        objs = []

        def add(d):
            if d is None or k.get(d.chan, 0) >= d.idx:
                return
            if need.get(d.chan, 0) < d.idx:
                need[d.chan] = d.idx
            objs.append(d)

        for d in self.pending[eng]:
            add(d)
        self.pending[eng] = []
        for b in reads:
            add(b.w)
        for b in writes:
            for d in [b.w] + list(b.r.values()):
                if d is None:
                    continue
                if d.chan == eng and not dma and eng == "pe":
                    continue
                add(d)
        if dma:
            q = self.dma_rr[eng]
            self.dma_rr[eng] = (q + 1) % self.NDMA
            o.chan = f"d_{eng}_{q}"
            add(self.dma_last.get(o.chan))
        else:
            o.chan = eng
        o.waits = list(need.items())
        for chan, idx in o.waits:
            k[chan] = max(k.get(chan, 0), idx)
        for d in objs:
            for c2, i2 in d.clock.items():
                if k.get(c2, 0) < i2:
                    k[c2] = i2
        self.cnt[o.chan] = self.cnt.get(o.chan, 0) + 1
        o.idx = self.cnt[o.chan]
        o.clock = dict(k)
        o.clock[o.chan] = o.idx
        if dma:
            self.dma_last[o.chan] = o
        for b in reads:
            b.r[o.chan] = o
        for b in writes:
            b.w = o
            b.r = {}
        self.last[o.chan] = o
        self.ops[eng].append(o)
        self.all_ops.append(o)
        return o

    def finish(self, final_ops):
        nc = self.nc
        byid = {}
        for e, lst in self.ops.items():
            for o in lst:
                byid[(o.chan, o.idx)] = o
        for e, lst in self.ops.items():
            for o in lst:
                for w in o.waits:
                    byid[w].sig = True
        for o in final_ops:
            o.sig = True
        val = {}
        run = {}
        for o in self.all_ops:
            if o.dma:
                val[(o.chan, o.idx)] = 16 * o.idx
        for e, lst in self.ops.items():
            c = 0
            for o in lst:
                if not o.dma:
                    if o.sig:
                        c += 1
                    val[(o.chan, o.idx)] = c if o.sig else None
        chans = sorted(self.cnt.keys())
        sems = {c: nc.alloc_semaphore(name=f"s_{c}") for c in chans}
        self.maxval = max([v for v in val.values() if v is not None] + [0])
        with nc.Block() as block:
            def emit(eng_name):
                def body(eng):
                    for o in self.ops[eng_name]:
                        for w in o.waits:
                            eng.wait_ge(sems[w[0]], val[w])
                        ins = o.fn(eng)
                        if o.sig:
                            ins.then_inc(sems[o.chan], 16 if o.dma else 1)
                    for o in final_ops:
                        if o.eng == eng_name:
                            eng.wait_ge(sems[o.chan], val[(o.chan, o.idx)])
                return body
            for name in self.engs:
                if self.ops[name]:
                    getattr(block, {"pe": "tensor", "act": "scalar", "dve": "vector", "pool": "gpsimd", "sp": "sync"}[name])(emit(name))


D = 1024
L = 4096
LC = 256
T = L + LC
FF = 2816
NFF = FF // 128
DEPTH = 2
EPS = 1.0000001e-6
TILES = [(0, 256)] + [(256 + 512 * i, 512) for i in range(8)]
DIN = 5568


class TT:
    __slots__ = ("ap", "buf")

    def __init__(self, ap, buf):
        self.ap, self.buf = ap, buf

    def __getitem__(self, key):
        return TT(self.ap[key], self.buf)

    def re(self, pattern, **kw):
        return TT(self.ap.rearrange(pattern, **kw), self.buf)


class DT:
    def __init__(self, ap, name):
        self.ap, self.name, self.bufs = ap, name, {}

    def t(self, key, ap=None):
        if key not in self.bufs:
            self.bufs[key] = Buf(f"{self.name}_{key}")
        return TT(self.ap if ap is None else ap, self.bufs[key])


class Builder:
    ARENA_WORDS = 52000

    def __init__(self):
        self.nc = nc = bass.Bass("TRN2", target_bir_lowering=False)
        self.S = Sched(nc)
        self.arena = nc.alloc_sbuf_tensor("arena", [128, self.ARENA_WORDS], F32)
        self.off = 0
        self.base = 0
        self.psum = [TT(nc.alloc_psum_tensor(f"ps{i}", [128, 512], F32)[:, :], Buf(f"ps{i}")) for i in range(8)]
        self.nbuf = 0

    def alloc(self, n, dtype=F32, name=None):
        words = (n * (2 if dtype == BF16 else 4) + 3) // 4
        assert self.off + words <= self.ARENA_WORDS, (self.off, words)
        ap = self.arena[:, self.off:self.off + words]
        self.off += words
        if dtype == BF16:
            ap = ap.bitcast(BF16)[:, 0:n]
        self.nbuf += 1
        return TT(ap, Buf(name or f"b{self.nbuf}"))

    def persist(self):
        self.base = self.off

    def new_stage(self):
        self.S.barrier()
        self.off = self.base

    def dram_in(self, name, shape):
        return DT(self.nc.dram_tensor(name, list(shape), F32, kind="ExternalInput").ap(), name)

    def dram_tmp(self, name, shape, dtype):
        return DT(self.nc.dram_tensor(name, list(shape), dtype).ap(), name)

    def dma(self, q, out, in_):
        return self.S.op(q, lambda e: e.dma_start(out=out.ap, in_=in_.ap), [in_.buf], [out.buf], dma=True)

    def mm(self, out, lhsT, rhs, start=True, stop=True):
        return self.S.op("pe", lambda e: e.matmul(out.ap, lhsT.ap, rhs.ap, start=start, stop=stop),
                         [lhsT.buf, rhs.buf], [out.buf])

    def tr(self, out, in_, ident):
        return self.S.op("pe", lambda e: e.transpose(out.ap, in_.ap, ident.ap), [in_.buf, ident.buf], [out.buf])

    def act(self, out, in_, func, bias=None, scale=None, eng="act"):
        rd = [in_.buf]
        kw = {}
        if bias is not None:
            if isinstance(bias, TT):
                rd.append(bias.buf); kw["bias"] = bias.ap
            else:
                kw["bias"] = bias
        if scale is not None:
            if isinstance(scale, TT):
                rd.append(scale.buf); kw["scale"] = scale.ap
            else:
                kw["scale"] = scale
        return self.S.op("act", lambda e: e.activation(out.ap, in_.ap, func, **kw), rd, [out.buf])

    def copy(self, eng, out, in_):
        if eng == "act":
            return self.S.op(eng, lambda e: e.activation(out.ap, in_.ap, AF.Copy), [in_.buf], [out.buf])
        return self.S.op(eng, lambda e: e.tensor_copy(out.ap, in_.ap), [in_.buf], [out.buf])

    def tt(self, eng, out, in0, in1, op):
        return self.S.op(eng, lambda e: e.tensor_tensor(out.ap, in0.ap, in1.ap, op), [in0.buf, in1.buf], [out.buf])

    def ts(self, eng, out, in0, s1, s2, op0, op1=None):
        rd = [in0.buf]
        a1 = s1
        a2 = s2
        if isinstance(s1, TT):
            rd.append(s1.buf); a1 = s1.ap
        if isinstance(s2, TT):
            rd.append(s2.buf); a2 = s2.ap
        if op1 is None:
            return self.S.op(eng, lambda e: e.tensor_scalar(out.ap, in0.ap, a1, None, op0), rd, [out.buf])
        return self.S.op(eng, lambda e: e.tensor_scalar(out.ap, in0.ap, a1, a2, op0, op1), rd, [out.buf])

    def stt(self, eng, out, in0, scalar, in1, op0, op1):
        rd = [in0.buf, in1.buf]
        sc = scalar
        if isinstance(scalar, TT):
            rd.append(scalar.buf); sc = scalar.ap
        return self.S.op(eng, lambda e: e.scalar_tensor_tensor(out.ap, in0.ap, sc, in1.ap, op0, op1), rd, [out.buf])

    def memset(self, eng, out, v):
        return self.S.op(eng, lambda e: e.memset(out.ap, v), [], [out.buf])


NVL = 36
WNAMES = ["w_ada", "ffn1_w1", "ffn1_w3", "ffn1_w2", "w_in", "w_uq", "w_uq_rot", "w_ukv", "w_pool", "w_a2", "w_branch",
          "w_out", "ffn2_w1", "ffn2_w3", "ffn2_w2"]


class Model:
    def __init__(self, cfg):
        self.cfg = cfg
        self.B = B = Builder()
        self.nc = B.nc
        nc = B.nc
        self.x_in = B.dram_in("x", [L, D])
        self.ctx_in = B.dram_in("ctx", [LC, D])
        self.cvec = B.dram_in("cvec", [128, 16])
        self.vecs_in = B.dram_in("vecs", [128, DEPTH * NVL])
        self.bada_in = B.dram_in("bada", [128, DEPTH * 72])
        self.idn_in = B.dram_in("idn", [128, 128])
        self.W = {}
        shapes = {"w_ada": [DEPTH, D, 9 * D], "ffn1_w1": [DEPTH, D, FF], "ffn1_w3": [DEPTH, D, FF], "ffn1_w2": [DEPTH, FF, D],
                  "ffn2_w1": [DEPTH, D, FF], "ffn2_w3": [DEPTH, D, FF], "ffn2_w2": [DEPTH, FF, D]}
        shapes.update({"w_in": [DEPTH, D, DIN], "w_uq": [DEPTH, 256, 768], "w_uq_rot": [DEPTH, 256, 768],
                       "wuk_pad": [DEPTH, 128, 768], "wuv": [DEPTH, 128, 512], "w_pool": [DEPTH, 4, 128, 128],
                       "w_a2": [DEPTH, 2, 16, 256], "b_a2": [DEPTH, 2, 128, 256], "w_branch": [DEPTH, 3, 512, D],
                       "w_out": [DEPTH, D, D], "sel": [2, 32, 96], "cos": [96, L], "sin": [96, L],
                       "band": [4, 5, 128, 128], "tri": [2, 3, 128, 128], "mask": [2, 128, 256]})
        for k, v in shapes.items():
            self.W[k] = B.dram_in(k, v)
        dbg = cfg.get("debug", False)

        def scratch(name, shape, dtype):
            if dbg:
                return DT(nc.dram_tensor(name, list(shape), dtype, kind="ExternalOutput").ap(), name)
            return B.dram_tmp(name, shape, dtype)
        self.uT_d = scratch("uT_d", [D, T], BF16)
        self.qfT = scratch("qfT", [8, 96, T], BF16)
        self.kfT = scratch("kfT", [8, 96, T], BF16)
        self.v_tm = scratch("v_tm", [T, 512], BF16)
        self.pz_tm = scratch("pz_tm", [T, 512], BF16)
        self.gqkT = scratch("gqkT", [512, T], BF16)
        self.gkv_tm = scratch("gkv_tm", [T, 768], BF16)
        self.gaT = scratch("gaT", [2, 16, T], BF16)
        self.grT = scratch("grT", [512, T], BF16)
        self.oaT = scratch("oaT", [512, T], BF16)
        self.opT = scratch("opT", [512, T], BF16)
        self.ogT = scratch("ogT", [512, T], BF16)
        self.ofT = scratch("ofT", [512, T], F32)
        self.out = DT(nc.dram_tensor("out", [L, D], F32, kind="ExternalOutput").ap(), "out")
        self.hT = B.dram_tmp("hT", [D, T], F32)
        self.ident = B.alloc(128)
        self.ones_bf = B.alloc(128, BF16)
        self.ones_f = B.alloc(64)
        self.vecs = B.alloc(DEPTH * NVL)
        self.bada = B.alloc(DEPTH * 72)
        self.modv = B.alloc(DEPTH * 2 * 72)
        self.dv = B.alloc(DEPTH * 2 * 72)
        B.persist()
        B.dma("sp", self.ident, self.idn_in.t(0))
        B.dma("sp", self.vecs, self.vecs_in.t(0))
        B.dma("sp", self.bada, self.bada_in.t(0))
        B.memset("dve", self.ones_bf, 1.0)
        B.memset("dve", self.ones_f, 1.0)
        self.finals = []

    def vec(self, l, col, n=1, rows=128):
        return self.vecs[0:rows, l * NVL + col: l * NVL + col + n]

    def mod(self, l, s, m):
        o = (l * 2 + s) * 72 + m * 8
        return self.modv[:, o:o + 8]

    def dvv(self, l, s, m):
        o = (l * 2 + s) * 72 + m * 8
        return self.dv[:, o:o + 8]

    def stage_xin(self):
        B = self.B
        B.new_stage()
        xs = [B.alloc(4 * D) for _ in range(2)]
        hs = [B.alloc(8 * 512) for _ in range(2)]
        hTv = self.hT.ap.rearrange("(c p) t -> p c t", p=128)
        for ti, (t0, n) in enumerate(TILES):
            if self.cfg.get("quick") and ti > 1:
                break
            nb = n // 128
            x_t = xs[ti % 2].re("p (b d) -> p b d", b=4)
            h_t = hs[ti % 2].re("p (c t) -> p c t", c=8)
            if ti == 0:
                src = self.ctx_in.t(0, self.ctx_in.ap.rearrange("(b p) d -> p b d", p=128))
            else:
                src = self.x_in.t(ti, self.x_in.ap[t0 - LC:t0 - LC + n, :].rearrange("(b p) d -> p b d", p=128))
            B.dma("sp", x_t[:, 0:nb, :], src)
            for c in range(8):
                ps = B.psum[c]
                for b in range(nb):
                    B.tr(ps[:, b * 128:(b + 1) * 128], x_t[:, b, c * 128:(c + 1) * 128], self.ident)
                B.copy("dve" if c % 2 == 0 else "act", h_t[:, c, 0:n], ps[:, 0:n])
            B.dma("sp", self.hT.t(ti, hTv[:, :, t0:t0 + n]), h_t[:, :, 0:n])

    def stage_xout(self):
        B = self.B
        B.new_stage()
        hs = [B.alloc(8 * 512) for _ in range(2)]
        os_ = [B.alloc(4 * D) for _ in range(2)]
        hTv = self.hT.ap.rearrange("(c p) t -> p c t", p=128)
        k = 0
        for ti, (t0, n) in enumerate(TILES):
            if ti == 0 or (self.cfg.get("quick") and ti > 1):
                continue
            h_t = hs[ti % 2].re("p (c t) -> p c t", c=8)
            o_t = os_[ti % 2].re("p (b d) -> p b d", b=4)
            B.dma("sp", h_t, self.hT.t(ti, hTv[:, :, t0:t0 + n]))
            for b in range(4):
                for half in range(2):
                    ps = B.psum[k % 8]
                    k += 1
                    for cc in range(4):
                        c = half * 4 + cc
                        B.tr(ps[:, cc * 128:(cc + 1) * 128], h_t[:, c, b * 128:(b + 1) * 128], self.ident)
                    B.copy("dve" if k % 2 == 0 else "act", o_t[:, b, half * 512:(half + 1) * 512], ps)
            dst = self.out.t(ti, self.out.ap[t0 - LC:t0 - LC + n, :].rearrange("(b p) d -> p b d", p=128))
            self.finals.append(B.dma("sp", dst, o_t))

    def stage_adaln(self):
        B = self.B
        B.new_stage()
        cv = B.alloc(16)
        sc = B.alloc(16)
        B.dma("sp", cv, self.cvec.t(0))
        B.act(sc, cv, AF.Silu)
        sc3 = sc.re("p (c s) -> p c s", s=2)
        wb = [B.alloc(8 * 512) for _ in range(3)]
        k = 0
        for l in range(DEPTH):
            mps = B.psum[l][:, 0:144]
            for j in range(18):
                w_t = wb[k % 3].re("p (c n) -> p c n", c=8)
                k += 1
                src = self.W["w_ada"].t(0, self.W["w_ada"].ap[l, :, j * 512:(j + 1) * 512].rearrange("(c p) n -> p c n", p=128))
                B.dma("sp" if k % 2 == 0 else "act", w_t, src)
                for i in range(4):
                    col = (j * 4 + i) * 2
                    for c in range(8):
                        B.mm(mps[:, col:col + 2], w_t[:, c, i * 128:(i + 1) * 128], sc3[:, c, :], start=(c == 0), stop=(c == 7))
            m3 = mps.re("p (i s) -> p i s", s=2)
            for s in range(2):
                o = (l * 2 + s) * 72
                B.tt("dve", self.modv[:, o:o + 72], m3[:, :, s], self.bada[:, l * 72:(l + 1) * 72], ALU.add)
            for s in range(2):
                for (m_scale, gcol, m_out) in ((1, 0, 1), (4, 8, 4), (7, 16, 7)):
                    B.stt("dve", self.dvv(l, s, m_out), self.mod(l, s, m_scale), 1.0, self.vec(l, gcol, 8), ALU.add, ALU.mult)
                for m in (2, 8):
                    B.ts("dve", self.dvv(l, s, m), self.mod(l, s, m), 0.5, None, ALU.mult)

    def load_cast(self, dst, src_ap, src_dt, stg, k):
        B = self.B
        s = stg[k % len(stg)]
        n = dst.ap.shape[-1] if len(dst.ap.shape) == 2 else None
        B.dma("sp" if k % 2 == 0 else "act", s, src_dt.t(0, src_ap))
        B.copy("pool", dst, s)

    def stage_ffn(self, l, which, tiles):
        B = self.B
        B.new_stage()
        pre = "ffn1" if which == 1 else "ffn2"
        m0 = 0 if which == 1 else 6
        w1 = B.alloc(8 * FF, BF16).re("p (c f) -> p c f", c=8)
        w3 = B.alloc(8 * FF, BF16).re("p (c f) -> p c f", c=8)
        w2 = B.alloc(NFF * D, BF16).re("p (f d) -> p f d", f=NFF)
        stg = [B.alloc(1024) for _ in range(2)]
        k = 0
        for c in range(8):
            for (wt, nm) in ((w1, "_w1"), (w3, "_w3")):
                src = self.W[pre + nm]
                for (c0, cn) in ((0, 1024), (1024, 1024), (2048, 768)):
                    B.dma("sp" if k % 2 == 0 else "act", stg[k % 2][:, 0:cn], src.t(0, src.ap[l, c * 128:(c + 1) * 128, c0:c0 + cn]))
                    B.copy("pool", wt[:, c, c0:c0 + cn], stg[k % 2][:, 0:cn])
                    k += 1
        src = self.W[pre + "_w2"]
        for f in range(NFF):
            B.dma("sp" if k % 2 == 0 else "act", stg[k % 2], src.t(0, src.ap[l, f * 128:(f + 1) * 128, :]))
            B.copy("pool", w2[:, f, :], stg[k % 2])
            k += 1
        hs = B.alloc(8 * 512).re("p (c t) -> p c t", c=8)
        uT = B.alloc(8 * 512, BF16).re("p (c t) -> p c t", c=8)
        gT = B.alloc(NFF * 512, BF16).re("p (f t) -> p f t", f=NFF)
        sq = [B.alloc(512, BF16) for _ in range(2)]
        rs = B.alloc(512)
        tmp = [B.alloc(512)]
        sa = [B.alloc(512) for _ in range(2)]
        hTv = self.hT.ap.rearrange("(c p) t -> p c t", p=128)
        ps = B.psum
        ka = 0
        for ti in tiles:
            t0, n = TILES[ti]
            s = 0 if ti > 0 else 1
            gs, sh, hg = self.dvv(l, s, m0 + 1), self.mod(l, s, m0), self.dvv(l, s, m0 + 2)
            B.dma("sp", hs[:, :, 0:n], self.hT.t(ti, hTv[:, :, t0:t0 + n]))
            for c in range(8):
                B.act(sq[c % 2][:, 0:n], hs[:, c, 0:n], AF.Square)
                B.mm(ps[6][:, 0:n], self.ones_bf, sq[c % 2][:, 0:n], start=(c == 0), stop=(c == 7))
            B.act(rs[:, 0:n], ps[6][:, 0:n], AF.Sqrt, bias=EPS, scale=1.0 / D)
            B.S.op("dve", lambda e, o=rs[:, 0:n].ap: e.reciprocal(o, o), [rs.buf], [rs.buf])
            for c in range(8):
                B.stt("dve", tmp[0][:, 0:n], hs[:, c, 0:n], gs[:, c:c + 1], rs[:, 0:n], ALU.mult, ALU.mult)
                B.act(uT[:, c, 0:n], tmp[0][:, 0:n], AF.Identity, bias=sh[:, c:c + 1], scale=1.0)
            for f in range(NFF):
                pa, pb = ps[ka % 2], ps[2 + ka % 2]
                for c in range(8):
                    B.mm(pa[:, 0:n], w1[:, c, f * 128:(f + 1) * 128], uT[:, c, 0:n], start=(c == 0), stop=(c == 7))
                for c in range(8):
                    B.mm(pb[:, 0:n], w3[:, c, f * 128:(f + 1) * 128], uT[:, c, 0:n], start=(c == 0), stop=(c == 7))
                B.act(sa[ka % 2][:, 0:n], pa[:, 0:n], AF.Silu)
                B.tt("dve", gT[:, f, 0:n], sa[ka % 2][:, 0:n], pb[:, 0:n], ALU.mult)
                ka += 1
            for d in range(8):
                po = ps[4 + d % 2]
                for f in range(NFF):
                    B.mm(po[:, 0:n], w2[:, f, d * 128:(d + 1) * 128], gT[:, f, 0:n], start=(f == 0), stop=(f == NFF - 1))
                B.stt("dve", hs[:, d, 0:n], po[:, 0:n], hg[:, d:d + 1], hs[:, d, 0:n], ALU.mult, ALU.add)
            B.dma("sp", self.hT.t(ti, hTv[:, :, t0:t0 + n]), hs[:, :, 0:n])

    def wload(self, dst, src, ap):
        B = self.B
        k = self.wk
        self.wk += 1
        P, n = ap.shape[0], ap.shape[-1]
        st = self.stg[k % 2][0:P, 0:n]
        B.dma("sp" if k % 2 == 0 else "act", st, src.t(0, ap))
        B.copy("pool", dst, st)

    def norm_u(self, l, s, hs, uT, n, gs, sh, sq, rs, tmp):
        B = self.B
        ps = B.psum
        for c in range(8):
            B.act(sq[c % 2][:, 0:n], hs[:, c, 0:n], AF.Square)
            B.mm(ps[6][:, 0:n], self.ones_bf, sq[c % 2][:, 0:n], start=(c == 0), stop=(c == 7))
        B.act(rs[:, 0:n], ps[6][:, 0:n], AF.Sqrt, bias=EPS, scale=1.0 / D)
        B.S.op("dve", lambda e, o=rs[:, 0:n].ap: e.reciprocal(o, o), [rs.buf], [rs.buf])
        for c in range(8):
            B.stt("dve", tmp[:, 0:n], hs[:, c, 0:n], gs[:, c:c + 1], rs[:, 0:n], ALU.mult, ALU.mult)
            B.act(uT[:, c, 0:n], tmp[:, 0:n], AF.Identity, bias=sh[:, c:c + 1], scale=1.0)

    def stage_in(self, l):
        B = self.B
        B.new_stage()
        ps = B.psum
        self.stg = [B.alloc(1024) for _ in range(2)]
        self.wk = 0
        NA = 2496
        win = B.alloc(8 * NA, BF16).re("p (c f) -> p c f", c=8)
        wuq = B.alloc(2 * 768, BF16).re("p (c f) -> p c f", c=2)
        wuqr = B.alloc(2 * 768, BF16).re("p (c f) -> p c f", c=2)
        wuk = B.alloc(768, BF16)
        wuv = B.alloc(512, BF16)
        sel = B.alloc(192, BF16)
        W = self.W
        for c in range(8):
            for (c0, cn) in ((0, 1024), (1024, 1024), (2048, NA - 2048)):
                self.wload(win[:, c, c0:c0 + cn], W["w_in"], W["w_in"].ap[l, c * 128:(c + 1) * 128, c0:c0 + cn])
        for c in range(2):
            self.wload(wuq[:, c, :], W["w_uq"], W["w_uq"].ap[l, c * 128:(c + 1) * 128, :])
            self.wload(wuqr[:, c, :], W["w_uq_rot"], W["w_uq_rot"].ap[l, c * 128:(c + 1) * 128, :])
        self.wload(wuk, W["wuk_pad"], W["wuk_pad"].ap[l])
        self.wload(wuv, W["wuv"], W["wuv"].ap[l])
        for i in range(2):
            self.wload(sel[0:32, i * 96:(i + 1) * 96], W["sel"], W["sel"].ap[i])
        hs = B.alloc(8 * 512).re("p (c t) -> p c t", c=8)
        uT = B.alloc(8 * 512, BF16).re("p (c t) -> p c t", c=8)
        sq = [B.alloc(512, BF16) for _ in range(2)]
        rs = B.alloc(512)
        tmp = B.alloc(512)
        cq_sb = B.alloc(2 * 512).re("p (c t) -> p c t", c=2)
        cqn = B.alloc(2 * 512, BF16).re("p (c t) -> p c t", c=2)
        ckv_sb = B.alloc(512)
        ckvn = B.alloc(512, BF16)
        kr_bf = B.alloc(512, BF16)
        krrot = B.alloc(512)
        cos_t = B.alloc(512)
        sin_t = B.alloc(512)
        gr_sb = B.alloc(4 * 512, BF16).re("p (c t) -> p c t", c=4)
        gqk_sb = B.alloc(4 * 512, BF16).re("p (c t) -> p c t", c=4)
        ga_sb = B.alloc(2 * 512, BF16).re("p (c t) -> p c t", c=2)
        qf_sb = B.alloc(8 * 512, BF16).re("p (c t) -> p c t", c=8)
        kf_sb = B.alloc(8 * 512, BF16).re("p (c t) -> p c t", c=8)
        v_sb = B.alloc(4 * 512, BF16).re("p (b f) -> p b f", b=4)
        pz_sb = B.alloc(4 * 512, BF16).re("p (b f) -> p b f", b=4)
        gkv_sb = B.alloc(4 * 768, BF16).re("p (b f) -> p b f", b=4)
        hsq = [B.alloc(512, BF16) for _ in range(2)]
        hrs = [B.alloc(512) for _ in range(2)]
        hn = [B.alloc(512) for _ in range(2)]
        hrn = [B.alloc(512) for _ in range(2)]
        ht1 = [B.alloc(512) for _ in range(2)]
        ht2 = [B.alloc(512) for _ in range(2)]
        hTv = self.hT.ap.rearrange("(c p) t -> p c t", p=128)
        kb = 0
        kt = 0
        kh = 0
        for ti in range(2 if self.cfg.get("quick") else 9):
            t0, n = TILES[ti]
            nb = n // 128
            s = 0 if ti > 0 else 1
            B.dma("sp", hs[:, :, 0:n], self.hT.t(ti, hTv[:, :, t0:t0 + n]))
            if ti > 0 and not self.cfg.get("no_cos"):
                B.dma("sp", cos_t[0:96, 0:n], W["cos"].t(0, W["cos"].ap[:, t0 - LC:t0 - LC + n]))
                B.dma("sp", sin_t[0:96, 0:n], W["sin"].t(0, W["sin"].ap[:, t0 - LC:t0 - LC + n]))
            self.norm_u(l, s, hs, uT, n, self.dvv(l, s, 4), self.mod(l, s, 3), sq, rs, tmp)
            if not self.cfg.get("no_ut"):
                B.dma("sp", self.uT_d.t(ti, self.uT_d.ap.rearrange("(c p) t -> p c t", p=128)[:, :, t0:t0 + n]), uT[:, :, 0:n])

            stop = self.cfg.get("in_stop", 9)
            if stop <= 1:
                continue

            def fm(col0, m):
                nonlocal kb
                p = ps[kb % 4]
                kb += 1
                for c in range(8):
                    B.mm(p[0:m, 0:n], win[:, c, col0:col0 + m], uT[:, c, 0:n], start=(c == 0), stop=(c == 7))
                return p
            sub2 = self.cfg.get("sub2", 9)
            for j in range(2):
                p = fm(j * 128, 128)
                B.copy("dve", cq_sb[:, j, 0:n], p[:, 0:n])
                if sub2 >= 2:
                    B.act(sq[j][:, 0:n], cq_sb[:, j, 0:n], AF.Square)
            if sub2 >= 3:
                for j in range(2):
                    B.mm(ps[6][:, 0:n], self.ones_bf, sq[j][:, 0:n], start=(j == 0), stop=(j == 1))
                B.act(rs[:, 0:n], ps[6][:, 0:n], AF.Sqrt, bias=EPS, scale=1.0 / 256)
                B.S.op("dve", lambda e, o=rs[:, 0:n].ap: e.reciprocal(o, o), [rs.buf], [rs.buf])
            if sub2 >= 4:
                for j in range(2):
                    B.stt("dve", cqn[:, j, 0:n], cq_sb[:, j, 0:n], self.vec(l, 28 + j), rs[:, 0:n], ALU.mult, ALU.mult)
            if sub2 < 9:
                continue
            p = fm(256, 128)
            B.copy("dve", ckv_sb[:, 0:n], p[:, 0:n])
            B.act(sq[0][:, 0:n], ckv_sb[:, 0:n], AF.Square)
            B.mm(ps[6][:, 0:n], self.ones_bf, sq[0][:, 0:n])
            B.act(rs[:, 0:n], ps[6][:, 0:n], AF.Sqrt, bias=EPS, scale=1.0 / 128)
            B.S.op("dve", lambda e, o=rs[:, 0:n].ap: e.reciprocal(o, o), [rs.buf], [rs.buf])
            B.stt("dve", ckvn[:, 0:n], ckv_sb[:, 0:n], self.vec(l, 30), rs[:, 0:n], ALU.mult, ALU.mult)
            if self.cfg.get("sub", 9) <= 1:
                continue
            p = fm(384, 32)
            B.copy("act", kr_bf[0:32, 0:n], p[0:32, 0:n])
            if ti > 0 and self.cfg.get("sub", 9) > 2:
                p = ps[kb % 4]
                kb += 1
                B.mm(p[0:96, 0:n], sel[0:32, 96:192], kr_bf[0:32, 0:n])
                B.copy("act", krrot[0:96, 0:n], p[0:96, 0:n])
            if stop <= 2:
                continue
            for h in range(8):
                for isk in range(2):
                    i2 = kh % 2
                    kh += 1
                    p = ps[kb % 4]
                    kb += 1
                    if isk == 0:
                        for j in range(2):
                            B.mm(p[0:96, 0:n], wuq[:, j, h * 96:(h + 1) * 96], cqn[:, j, 0:n], start=(j == 0), stop=(j == 1))
                        gcol, rcol, dst = 31, 32, qf_sb
                    else:
                        B.mm(p[0:96, 0:n], wuk[:, h * 96:(h + 1) * 96], ckvn[:, 0:n], start=True, stop=False)
                        B.mm(p[0:96, 0:n], sel[0:32, 0:96], kr_bf[0:32, 0:n], start=False, stop=True)
                        gcol, rcol, dst = 33, 34, kf_sb
                    B.copy("act", ht1[i2][0:96, 0:n], p[0:96, 0:n])
                    p = ht1[i2]
                    B.act(hsq[i2][0:96, 0:n], p[0:96, 0:n], AF.Square)
                    B.mm(ps[7][0:96, 0:n], self.ones_bf[0:96, 0:96], hsq[i2][0:96, 0:n])
                    B.act(hrs[i2][0:96, 0:n], ps[7][0:96, 0:n], AF.Sqrt, bias=EPS, scale=1.0 / 96)
                    B.S.op("dve", lambda e, o=hrs[i2][0:96, 0:n].ap: e.reciprocal(o, o), [hrs[i2].buf], [hrs[i2].buf])
                    if ti == 0:
                        B.stt("dve", dst[0:96, h, 0:n], p[0:96, 0:n], self.vec(l, gcol, 1, 96), hrs[i2][0:96, 0:n], ALU.mult, ALU.mult)
                        continue
                    B.stt("dve", hn[i2][0:96, 0:n], p[0:96, 0:n], self.vec(l, gcol, 1, 96), hrs[i2][0:96, 0:n], ALU.mult, ALU.mult)
                    if isk == 0:
                        p2 = ps[kb % 4]
                        kb += 1
                        for j in range(2):
                            B.mm(p2[0:96, 0:n], wuqr[:, j, h * 96:(h + 1) * 96], cqn[:, j, 0:n], start=(j == 0), stop=(j == 1))
                        rsrc = p2[0:96, 0:n]
                    else:
                        rsrc = krrot[0:96, 0:n]
                    B.stt("dve", hrn[i2][0:96, 0:n], rsrc, self.vec(l, rcol, 1, 96), hrs[i2][0:96, 0:n], ALU.mult, ALU.mult)
                    B.tt("dve", hn[i2][0:96, 0:n], hn[i2][0:96, 0:n], cos_t[0:96, 0:n], ALU.mult)
                    B.tt("dve", ht2[i2][0:96, 0:n], hrn[i2][0:96, 0:n], sin_t[0:96, 0:n], ALU.mult)
                    B.tt("dve", dst[0:96, h, 0:n], hn[i2][0:96, 0:n], ht2[i2][0:96, 0:n], ALU.add)
            B.dma("sp", self.qfT.t(ti, self.qfT.ap[:, :, t0:t0 + n].rearrange("h r t -> r h t")), qf_sb[0:96, :, 0:n])
            B.dma("sp", self.kfT.t(ti, self.kfT.ap[:, :, t0:t0 + n].rearrange("h r t -> r h t")), kf_sb[0:96, :, 0:n])
            if stop <= 3:
                continue
            for j in range(4):
                p = fm(928 + j * 128, 128)
                B.copy("act" if j % 2 else "dve", gqk_sb[:, j, 0:n], p[:, 0:n])
            B.dma("sp", self.gqkT.t(ti, self.gqkT.ap.rearrange("(c p) t -> p c t", p=128)[:, :, t0:t0 + n]), gqk_sb[:, :, 0:n])
            for j in range(2):
                p = fm(1952 + j * 16, 16)
                B.copy("act", ga_sb[0:16, j, 0:n], p[0:16, 0:n])
            B.dma("sp", self.gaT.t(ti, self.gaT.ap[:, :, t0:t0 + n].rearrange("d r t -> r d t")), ga_sb[0:16, :, 0:n])
            for j in range(4):
                p = fm(1984 + j * 128, 128)
                B.act(gr_sb[:, j, 0:n], p[:, 0:n], AF.Silu)
            B.dma("sp", self.grT.t(ti, self.grT.ap.rearrange("(c p) t -> p c t", p=128)[:, :, t0:t0 + n]), gr_sb[:, :, 0:n])
            if stop <= 4:
                continue
            for b in range(nb):
                bs = slice(b * 128, (b + 1) * 128)
                p = ps[4 + kt % 2]
                kt += 1
                B.mm(p[:, 0:512], ckvn[:, bs], wuv)
                B.copy("act", v_sb[:, b, :], p[:, 0:512])
                p = ps[4 + kt % 2]
                kt += 1
                for c in range(8):
                    B.mm(p[:, 0:512], uT[:, c, bs], win[:, c, 416:928], start=(c == 0), stop=(c == 7))
                B.copy("dve", pz_sb[:, b, :], p[:, 0:512])
                p = ps[4 + kt % 2]
                kt += 1
                for c in range(8):
                    B.mm(p[:, 0:256], uT[:, c, bs], win[:, c, 1184:1440], start=(c == 0), stop=(c == 7))
                B.copy("act", gkv_sb[:, b, 0:256], p[:, 0:256])
                p = ps[4 + kt % 2]
                kt += 1
                for c in range(8):
                    B.mm(p[:, 0:512], uT[:, c, bs], win[:, c, 1440:1952], start=(c == 0), stop=(c == 7))
                B.copy("dve", gkv_sb[:, b, 256:768], p[:, 0:512])
            B.dma("sp", self.v_tm.t(ti, self.v_tm.ap[t0:t0 + n, :].rearrange("(b p) f -> p b f", p=128)), v_sb[:, 0:nb, :])
            B.dma("sp", self.pz_tm.t(ti, self.pz_tm.ap[t0:t0 + n, :].rearrange("(b p) f -> p b f", p=128)), pz_sb[:, 0:nb, :])
            B.dma("sp", self.gkv_tm.t(ti, self.gkv_tm.ap[t0:t0 + n, :].rearrange("(b p) f -> p b f", p=128)), gkv_sb[:, 0:nb, :])

    def stage_att(self, l):
        B = self.B
        B.new_stage()
        ps = B.psum
        kf = [B.alloc(T, BF16) for _ in range(2)]
        qf = [B.alloc(T, BF16) for _ in range(2)]
        v_all = B.alloc(34 * 512, BF16).re("p (c f) -> p c f", c=34)
        vx = B.alloc(34 * 8 * 72, BF16).re("p (c h d) -> p c h d", c=34, h=8)
        pT = [B.alloc(512, BF16) for _ in range(3)]
        o_sb = [B.alloc(512) for _ in range(2)]
        oa_sb = [B.alloc(512, BF16) for _ in range(2)]
        B.memset("pool", vx, 1.0)
        vsrc = self.v_tm.ap.rearrange("(c p) f -> p c f", p=128)
        for c0 in range(0, 34, 9):
            c1 = min(c0 + 9, 34)
            B.dma("sp", v_all[:, c0:c1, :], self.v_tm.t("all", vsrc[:, c0:c1, :]))
        B.copy("pool", vx[:, :, :, 0:64], v_all.re("p c (h d) -> p c h d", h=8))
        qtiles = list(range(1, 9)) + ([0] if l < DEPTH - 1 else [])
        if self.cfg.get("quick"):
            qtiles = [1]
        kp = 0
        ko = 0
        sc = 96 ** -0.5
        for h in range(1 if self.cfg.get("quick") else 8):
            i2 = h % 2
            B.dma("sp", kf[i2][0:96, :], self.kfT.t("all", self.kfT.ap[h]))
            B.dma("sp", qf[i2][0:96, :], self.qfT.t("all", self.qfT.ap[h]))
            for ti in qtiles:
                t0, n = TILES[ti]
                nkc = 34 if ti > 0 else 2
                po = ps[6 + ko % 2]
                for kc in range(nkc):
                    sp_ = ps[kp % 4]
                    B.mm(sp_[:, 0:n], kf[i2][0:96, kc * 128:(kc + 1) * 128], qf[i2][0:96, t0:t0 + n])
                    B.act(pT[kp % 3][:, 0:n], sp_[:, 0:n], AF.Exp, scale=sc)
                    B.mm(po[0:66, 0:n], vx[:, kc, h, 0:66], pT[kp % 3][:, 0:n], start=(kc == 0), stop=(kc == nkc - 1))
                    kp += 1
                osb = o_sb[ko % 2]
                B.copy("dve", osb[0:65, 0:n], po[0:65, 0:n])
                B.S.op("dve", lambda e, o=osb[64:65, 0:n].ap: e.reciprocal(o, o), [osb.buf], [osb.buf])
                pb = ps[4 + ko % 2]
                B.mm(pb[0:64, 0:n], self.ones_f[64:65, 0:64], osb[64:65, 0:n])
                B.tt("dve", oa_sb[ko % 2][0:64, 0:n], osb[0:64, 0:n], pb[0:64, 0:n], ALU.mult)
                B.dma("sp", self.oaT.t((h, ti), self.oaT.ap[h * 64:(h + 1) * 64, t0:t0 + n]), oa_sb[ko % 2][0:64, 0:n])
                ko += 1

    def stage_pool(self, l):
        B = self.B
        B.new_stage()
        ps = B.psum
        W = self.W
        self.stg = [B.alloc(1024) for _ in range(2)]
        self.wk = 0
        pz = B.alloc(34 * 512, BF16).re("p (b f) -> p b f", b=34)
        band = B.alloc(20 * 128, BF16).re("p (g k t) -> p g k t", g=4, k=5)
        wp = B.alloc(4 * 128, BF16).re("p (g d) -> p g d", g=4)
        for g in range(4):
            for k in range(5):
                self.wload(band[:, g, k, :], W["band"], W["band"].ap[g, k])
            self.wload(wp[:, g, :], W["w_pool"], W["w_pool"].ap[l, g])
        psrc = self.pz_tm.ap.rearrange("(b p) f -> p b f", p=128)
        for c0 in range(0, 34, 9):
            c1 = min(c0 + 9, 34)
            B.dma("sp", pz[:, c0:c1, :], self.pz_tm.t("all", psrc[:, c0:c1, :]))
        pooled = [B.alloc(512, BF16) for _ in range(2)]
        op_sb = [B.alloc(4 * 512, BF16).re("p (g t) -> p g t", g=4) for _ in range(2)]
        tiles = list(range(1, 9)) + ([0] if l < DEPTH - 1 else [])
        if self.cfg.get("quick"):
            tiles = [1]
        kk = 0
        for ti in tiles:
            t0, n = TILES[ti]
            nb = n // 128
            b0 = t0 // 128
            first, last = (0, 1) if ti == 0 else (2, 33)
            osb = op_sb[kk % 2]
            for g in range(4):
                pp = ps[kk % 2 * 2]
                py = ps[kk % 2 * 2 + 1]
                gs = slice(g * 128, (g + 1) * 128)
                for bi in range(nb):
                    b = b0 + bi
                    terms = []
                    if b > first:
                        terms.append((b - 1, 0))
                    terms.append((b, 3 if b == first else (4 if b == last else 1)))
                    if b < last:
                        terms.append((b + 1, 2))
                    for i, (j, kind) in enumerate(terms):
                        B.mm(pp[:, bi * 128:(bi + 1) * 128], pz[:, j, gs], band[:, g, kind, :], start=(i == 0), stop=(i == len(terms) - 1))
                B.copy("act", pooled[g % 2][:, 0:n], pp[:, 0:n])
                B.mm(py[:, 0:n], wp[:, g, :], pooled[g % 2][:, 0:n])
                B.act(osb[:, g, 0:n], py[:, 0:n], AF.Identity, scale=self.vec(l, 24 + g))
            kk += 1
            B.dma("sp", self.opT.t(ti, self.opT.ap.rearrange("(g p) t -> p g t", p=128)[:, :, t0:t0 + n]), osb[:, :, 0:n])

    def stage_gla(self, l):
        B = self.B
        B.new_stage()
        ps = B.psum
        W = self.W
        self.stg = [B.alloc(1024) for _ in range(2)]
        self.wk = 0
        wa2 = B.alloc(2 * 256, BF16).re("p (d f) -> p d f", d=2)
        tri = B.alloc(6 * 128).re("p (d k t) -> p d k t", d=2, k=3)
        mask = B.alloc(2 * 256).re("p (d f) -> p d f", d=2)
        bias = B.alloc(2 * 256).re("p (d f) -> p d f", d=2)
        for d in range(2):
            self.wload(wa2[0:16, d, :], W["w_a2"], W["w_a2"].ap[l, d])
            for k in range(2):
                B.dma("sp", tri[:, d, k, :], W["tri"].t(0, W["tri"].ap[d, k]))
            B.dma("sp", mask[:, d, :], W["mask"].t(0, W["mask"].ap[d]))
            B.dma("sp", bias[:, d, :], W["b_a2"].t(0, W["b_a2"].ap[l, d]))
        S32 = B.alloc(512).re("p (h v) -> p h v", h=4)
        S16 = [B.alloc(512, BF16).re("p (h v) -> p h v", h=4) for _ in range(2)]
        qk = [B.alloc(8 * 128, BF16).re("p (w h t) -> p w h t", w=2, h=4) for _ in range(2)]
        gkv = [B.alloc(2 * 768, BF16).re("p (c f) -> p c f", c=2) for _ in range(2)]
        ga = [B.alloc(128, BF16) for _ in range(2)]
        gr = [B.alloc(4 * 128, BF16).re("p (c t) -> p c t", c=4) for _ in range(2)]
        of = [B.alloc(4 * 128).re("p (c t) -> p c t", c=4) for _ in range(2)]
        xb_ = [B.alloc(256) for _ in range(2)]
        xe_ = [B.alloc(256) for _ in range(2)]
        nla_ = [B.alloc(256) for _ in range(2)]
        eq_ = [B.alloc(512).re("p (h t) -> p h t", h=4) for _ in range(2)]
        ek_ = [B.alloc(512).re("p (h t) -> p h t", h=4) for _ in range(2)]
        qt_ = [B.alloc(512, BF16).re("p (h t) -> p h t", h=4) for _ in range(2)]
        ktt_ = [B.alloc(512, BF16).re("p (h t) -> p h t", h=4) for _ in range(2)]
        er_ = [B.alloc(512).re("p (c f) -> p c f", c=2) for _ in range(2)]
        kend_ = [B.alloc(512, BF16).re("p (c f) -> p c f", c=2) for _ in range(2)]
        at_ = [B.alloc(512, BF16) for _ in range(2)]
        osum = B.alloc(512)
        osq = B.alloc(512, BF16)
        ors = B.alloc(512)
        og1 = B.alloc(512)
        og_sb = [B.alloc(512, BF16).re("p (c t) -> p c t", c=4) for _ in range(2)]
        need_ctx_out = l < DEPTH - 1
        qksrc = self.gqkT.ap.rearrange("(w h k) t -> k w h t", w=2, h=4)
        for d in range(2):
            B.memset("dve", S32[0:64], 0.0)
            B.memset("dve", S16[0][0:64], 0.0)
            spar = 0
            blocks = list(range(34)) if d == 0 else ([1, 0] + list(range(33, 1, -1)))
            if self.cfg.get("quick"):
                blocks = blocks[0:4]
            def phase1(bi, b):
                i2 = bi % 2
                t0 = b * 128
                xb, xe, nla, eq, ek, qt, ktt, er, kend, at_sb = xb_[i2], xe_[i2], nla_[i2], eq_[i2], ek_[i2], qt_[i2], ktt_[i2], er_[i2], kend_[i2], at_[i2]
                B.dma("sp", qk[i2][0:64], self.gqkT.t("all", qksrc[:, :, :, t0:t0 + 128]))
                B.dma("sp", gkv[i2][0:64], self.gkv_tm.t("all", self.gkv_tm.ap[t0:t0 + 128, :].rearrange("(c p) f -> p c f", p=64)))
                B.dma("sp", ga[i2][0:16, :], self.gaT.t("all", self.gaT.ap[d, :, t0:t0 + 128]))
                want_out = (b >= 2) or need_ctx_out
                if d == 1 and want_out:
                    B.dma("sp", gr[i2], self.grT.t("all", self.grT.ap.rearrange("(c p) t -> p c t", p=128)[:, :, t0:t0 + 128]))
                    B.dma("sp", of[i2], self.ofT.t(b, self.ofT.ap.rearrange("(c p) t -> p c t", p=128)[:, :, t0:t0 + 128]))
                B.mm(ps[0][:, 0:256], ga[i2][0:16, :], wa2[0:16, d, :])
                B.tt("dve", xb, ps[0][:, 0:256], bias[:, d, :], ALU.add)
                B.act(xe, xb, AF.Exp, scale=-1.0)
                B.act(nla, xe, AF.Ln, bias=1.0, scale=1.0)
                for h in range(4):
                    B.mm(ps[1][0:64, h * 128:(h + 1) * 128], nla[:, h * 64:(h + 1) * 64], tri[:, d, 0, :])
                for cc in range(2):
                    B.mm(ps[2][0:64, cc * 256:(cc + 1) * 256], tri[:, d, 1, cc * 64:(cc + 1) * 64], nla)
                p1 = ps[1].re("p (h t) -> p h t", h=4)
                B.act(eq[0:64], p1[0:64], AF.Exp, scale=-1.0 / 16)
                B.act(ek[0:64], p1[0:64], AF.Exp, scale=1.0 / 16)
                B.stt("dve", qt[0:64], qk[i2][0:64, 0], 0.125, eq[0:64], ALU.mult, ALU.mult)
                B.tt("dve", ktt[0:64], qk[i2][0:64, 1], ek[0:64], ALU.mult)
                B.act(er[0:64], ps[2].re("p (c f) -> p c f", c=2)[0:64], AF.Exp, scale=-1.0 / 16)
                B.tt("dve", kend[0:64], gkv[i2][0:64, :, 0:256], er[0:64], ALU.mult)
                for cc in range(2):
                    cs = slice(cc * 64, (cc + 1) * 64)
                    for h in range(4):
                        o0 = (cc * 4 + h) * 64
                        B.mm(ps[3][0:64, o0:o0 + 64], ktt[0:64, h, cs], qt[0:64, h, cs])
                for cc in range(2):
                    B.tt("dve", at_sb[0:64, cc * 256:(cc + 1) * 256], ps[3][0:64, cc * 256:(cc + 1) * 256], mask[0:64, d, :], ALU.mult)

            def phase2(bi, b):
                nonlocal spar
                i2 = bi % 2
                t0 = b * 128
                want_out = (b >= 2) or need_ctx_out
                eq, qt, kend, at_sb = eq_[i2], qt_[i2], kend_[i2], at_[i2]
                po = ps[4 + bi % 2]
                for cc in ([0, 1] if d == 0 else [1, 0]):
                    cs = slice(cc * 64, (cc + 1) * 64)
                    endcol = cc * 64 + (63 if d == 0 else 0)
                    for h in range(4):
                        oc = slice(h * 128 + cc * 64, h * 128 + (cc + 1) * 64)
                        vv = gkv[i2][0:64, cc, 256 + h * 128:256 + (h + 1) * 128]
                        if want_out:
                            o0 = (cc * 4 + h) * 64
                            B.mm(po[:, oc], vv, at_sb[0:64, o0:o0 + 64], start=True, stop=False)
                            B.mm(po[:, oc], S16[spar][0:64, h, :], qt[0:64, h, cs], start=False, stop=True)
                        B.mm(ps[6][0:64, h * 128:(h + 1) * 128], kend[0:64, cc, h * 64:(h + 1) * 64], vv)
                    p6 = ps[6].re("p (h v) -> p h v", h=4)
                    for h in range(4):
                        B.stt("dve", S32[0:64, h, :], S32[0:64, h, :], eq[0:64, h, endcol:endcol + 1], p6[0:64, h, :], ALU.mult, ALU.add)
                    spar ^= 1
                    B.copy("act", S16[spar][0:64], S32[0:64])
                if not want_out:
                    return
                po4 = po.re("p (c t) -> p c t", c=4)
                if d == 0:
                    B.copy("dve", of[i2], po4)
                    B.dma("sp", self.ofT.t(b, self.ofT.ap.rearrange("(c p) t -> p c t", p=128)[:, :, t0:t0 + 128]), of[i2])
                else:
                    B.tt("dve", osum.re("p (c t) -> p c t", c=4), po4, of[i2], ALU.add)
                    B.act(osq, osum, AF.Square)
                    B.mm(ps[7], self.ones_bf, osq)
                    B.act(ors, ps[7], AF.Sqrt, bias=EPS, scale=1.0 / 128)
                    B.S.op("dve", lambda e, o=ors.ap: e.reciprocal(o, o), [ors.buf], [ors.buf])
                    B.stt("dve", og1, osum, self.vec(l, 35), ors, ALU.mult, ALU.mult)
                    B.tt("dve", og_sb[i2], og1.re("p (c t) -> p c t", c=4), gr[i2], ALU.mult)
                    B.dma("sp", self.ogT.t(b, self.ogT.ap.rearrange("(c p) t -> p c t", p=128)[:, :, t0:t0 + 128]), og_sb[i2])

            phase1(0, blocks[0])
            for bi, b in enumerate(blocks):
                if bi + 1 < len(blocks):
                    phase1(bi + 1, blocks[bi + 1])
                phase2(bi, b)

    def stage_merge(self, l):
        B = self.B
        B.new_stage()
        ps = B.psum
        W = self.W
        self.stg = [B.alloc(1024) for _ in range(2)]
        self.wk = 0
        wg = B.alloc(8 * 3072, BF16).re("p (c f) -> p c f", c=8)
        wbr = B.alloc(12 * D, BF16).re("p (r c f) -> p r c f", r=3, c=4)
        wout = B.alloc(8 * D, BF16).re("p (c f) -> p c f", c=8)
        for c in range(8):
            for q3 in range(3):
                self.wload(wg[:, c, q3 * 1024:(q3 + 1) * 1024], W["w_in"], W["w_in"].ap[l, c * 128:(c + 1) * 128, 2496 + q3 * 1024:2496 + (q3 + 1) * 1024])
            self.wload(wout[:, c, :], W["w_out"], W["w_out"].ap[l, c * 128:(c + 1) * 128, :])
        for r in range(3):
            for c in range(4):
                self.wload(wbr[:, r, c, :], W["w_branch"], W["w_branch"].ap[l, r, c * 128:(c + 1) * 128, :])
        hs = B.alloc(8 * 512).re("p (c t) -> p c t", c=8)
        uT = B.alloc(8 * 512, BF16).re("p (c t) -> p c t", c=8)
        ob = [B.alloc(4 * 512, BF16).re("p (c t) -> p c t", c=4) for _ in range(3)]
        mT = B.alloc(8 * 512, BF16).re("p (c t) -> p c t", c=8)
        sg = [B.alloc(512) for _ in range(2)]
        macc = B.alloc(512)
        mtmp = [B.alloc(512) for _ in range(2)]
        hTv = self.hT.ap.rearrange("(c p) t -> p c t", p=128)
        tiles = list(range(1, 9)) + ([0] if l < DEPTH - 1 else [])
        srcs = [self.oaT, self.opT, self.ogT]
        kk = 0
        if self.cfg.get("quick"):
            tiles = [1]
        for ti in tiles:
            t0, n = TILES[ti]
            s = 0 if ti > 0 else 1
            B.dma("sp", hs[:, :, 0:n], self.hT.t(ti, hTv[:, :, t0:t0 + n]))
            B.dma("sp", uT[:, :, 0:n], self.uT_d.t("all", self.uT_d.ap.rearrange("(c p) t -> p c t", p=128)[:, :, t0:t0 + n]))
            for r in range(3):
                B.dma("sp", ob[r][:, :, 0:n], srcs[r].t("all", srcs[r].ap.rearrange("(c p) t -> p c t", p=128)[:, :, t0:t0 + n]))
            for dch in range(8):
                ds_ = slice(dch * 128, (dch + 1) * 128)
                for r in range(3):
                    pg = ps[kk % 2]
                    pp = ps[2 + kk % 2]
                    gc = r * 1024 + dch * 128
                    for c in range(8):
                        B.mm(pg[:, 0:n], wg[:, c, gc:gc + 128], uT[:, c, 0:n], start=(c == 0), stop=(c == 7))
                    for c in range(4):
                        B.mm(pp[:, 0:n], wbr[:, r, c, ds_], ob[r][:, c, 0:n], start=(c == 0), stop=(c == 3))
                    B.act(sg[kk % 2][:, 0:n], pg[:, 0:n], AF.Sigmoid)
                    if r == 0:
                        B.tt("dve", macc[:, 0:n], pp[:, 0:n], sg[kk % 2][:, 0:n], ALU.mult)
                    else:
                        B.tt("dve", mtmp[kk % 2][:, 0:n], pp[:, 0:n], sg[kk % 2][:, 0:n], ALU.mult)
                        if r == 1:
                            B.tt("dve", macc[:, 0:n], macc[:, 0:n], mtmp[kk % 2][:, 0:n], ALU.add)
                        else:
                            B.tt("dve", mT[:, dch, 0:n], macc[:, 0:n], mtmp[kk % 2][:, 0:n], ALU.add)
                    kk += 1
            gm = self.mod(l, s, 5)
            for d2 in range(8):
                py = ps[4 + d2 % 2]
                for dch in range(8):
                    B.mm(py[:, 0:n], wout[:, dch, d2 * 128:(d2 + 1) * 128], mT[:, dch, 0:n], start=(dch == 0), stop=(dch == 7))
                B.stt("dve", hs[:, d2, 0:n], py[:, 0:n], gm[:, d2:d2 + 1], hs[:, d2, 0:n], ALU.mult, ALU.add)
            B.dma("sp", self.hT.t(ti, hTv[:, :, t0:t0 + n]), hs[:, :, 0:n])

    def build(self):
        cfg = self.cfg
        st = cfg["stages"]
        self.stage_xin()
        if "noada" not in st:
            self.stage_adaln()
        for l in range(cfg.get("nlayers", DEPTH)):
            last = l == DEPTH - 1
            if "ffn1" in st:
                self.stage_ffn(l, 1, list(range(9)))
            if "in" in st:
                self.stage_in(l)
            if "att" in st:
                self.stage_att(l)
            if "pool" in st:
                self.stage_pool(l)
            if "gla" in st:
                self.stage_gla(l)
            if "merge" in st:
                self.stage_merge(l)
            if "ffn2" in st:
                self.stage_ffn(l, 2, list(range(1, 9)) + ([] if last else [0]))
        self.stage_xout()
        self.B.S.finish(self.finals)
        return self.nc


def host_inputs(inp, b):
    f = np.float32
    m = {}
    m["x"] = np.ascontiguousarray(inp["x"][b])
    m["ctx"] = np.ascontiguousarray(inp["ctx"][b])
    cv = np.zeros((128, 8, 2), f)
    cv[:, :, 0] = inp["c"][b].reshape(8, 128).T
    cv[:, :, 1] = inp["c_ctx"].reshape(8, 128).T
    m["cvec"] = cv.reshape(128, 16)
    vec = np.zeros((128, DEPTH, NVL), f)
    perm = np.arange(32).reshape(2, 2, 8)[:, ::-1, :].reshape(32)
    for l in range(DEPTH):
        vec[:, l, 0:8] = inp["g_ffn1"][l].reshape(8, 128).T
        vec[:, l, 8:16] = inp["g_mix"][l].reshape(8, 128).T
        vec[:, l, 16:24] = inp["g_ffn2"][l].reshape(8, 128).T
        vec[:, l, 24:28] = inp["pool_scale"][l].reshape(4, 128).T
        vec[:, l, 28:30] = inp["g_cq"][l].reshape(2, 128).T
        vec[:, l, 30] = inp["g_ckv"][l]
        vec[:96, l, 31] = inp["g_qn"][l]
        vec[64:96, l, 32] = inp["g_qn"][l][64:][perm]
        vec[:96, l, 33] = inp["g_kn"][l]
        vec[64:96, l, 34] = inp["g_kn"][l][64:][perm]
        vec[:, l, 35] = inp["g_gla_o"][l]
    m["vecs"] = vec.reshape(128, DEPTH * NVL)
    m["bada"] = np.ascontiguousarray(inp["b_ada"].reshape(DEPTH, 72, 128).transpose(2, 0, 1).reshape(128, DEPTH * 72))
    m["idn"] = np.eye(128, dtype=f)
    for k in ("w_ada", "ffn1_w1", "ffn1_w3", "ffn1_w2", "ffn2_w1", "ffn2_w3", "ffn2_w2", "w_in", "w_uq", "w_pool", "w_a2",
              "w_branch", "w_out"):
        m[k] = inp[k]
    m.update(_consts())
    w_uq, w_ukv = inp["w_uq"], inp["w_ukv"]
    rot = np.zeros((DEPTH, 256, 768), f)
    kpad = np.zeros((DEPTH, 128, 768), f)
    wuv = np.zeros((DEPTH, 128, 512), f)
    for h in range(8):
        rot[:, :, h * 96 + 64:(h + 1) * 96] = w_uq[:, :, h * 96 + 64 + perm]
        kpad[:, :, h * 96:h * 96 + 64] = w_ukv[:, :, h * 128:h * 128 + 64]
        wuv[:, :, h * 64:(h + 1) * 64] = w_ukv[:, :, h * 128 + 64:(h + 1) * 128]
    m["w_uq_rot"], m["wuk_pad"], m["wuv"] = rot, kpad, wuv
    m["b_a2"] = np.ascontiguousarray(np.broadcast_to(inp["b_a2"][:, :, None, :], (DEPTH, 2, 128, 256)))
    return m


_CONSTS = None


def _consts():
    global _CONSTS
    if _CONSTS is not None:
        return _CONSTS
    f = np.float32
    c = {}
    perm = np.arange(32).reshape(2, 2, 8)[:, ::-1, :].reshape(32)
    sel = np.zeros((2, 32, 96), f)
    for r in range(32):
        sel[0, r, 64 + r] = 1.0
        sel[1, perm[r], 64 + r] = 1.0
    c["sel"] = sel
    t = np.arange(L)
    pos = np.stack([(t // 64).astype(f), (t % 64).astype(f)], 0)
    inv = (f(10000.0) ** (-np.arange(0, 16, 2, dtype=f) / f(16))).astype(f)
    ang = (pos[:, None, :] * inv[None, :, None]).astype(f)
    cos = np.ones((96, L), f)
    sin = np.zeros((96, L), f)
    for a in range(2):
        for half in range(2):
            r0 = 64 + a * 16 + half * 8
            cos[r0:r0 + 8] = np.cos(ang[a])
            sin[r0:r0 + 8] = np.sin(ang[a]) * (-1.0 if half == 0 else 1.0)
    c["cos"], c["sin"] = cos, sin
    band = np.zeros((4, 5, 128, 128), f)
    for g, w in enumerate((2, 4, 8, 16)):
        n3 = 384
        A = np.zeros((n3, n3), np.float64)
        for tt_ in range(n3):
            lo, hi = max(tt_ - w // 2, 0), min(tt_ + w // 2, n3)
            A[lo:hi, tt_] = 1.0 / (hi - lo)
            A[tt_, tt_] -= 1.0
        band[g, 0] = A[0:128, 128:256]
        band[g, 1] = A[128:256, 128:256]
        band[g, 2] = A[256:384, 128:256]
        band[g, 3] = A[0:128, 0:128]
        band[g, 4] = A[256:384, 256:384]
    c["band"] = band
    tri = np.zeros((2, 3, 128, 128), f)
    mask = np.zeros((2, 128, 256), f)
    for tp in range(128):
        for t_ in range(128):
            if tp // 64 != t_ // 64:
                continue
            tri[0, 0, tp, t_] = 1.0 if tp <= t_ else 0.0
            tri[1, 0, tp, t_] = 1.0 if tp >= t_ else 0.0
            tri[0, 1, tp, t_] = 1.0 if tp > t_ else 0.0
            tri[1, 1, tp, t_] = 1.0 if tp < t_ else 0.0
    for cc in range(2):
        for j_ in range(64):
            for h in range(4):
                for i_ in range(64):
                    mask[0, cc * 64 + j_, h * 64 + i_] = 1.0 if j_ <= i_ else 0.0
                    mask[1, cc * 64 + j_, h * 64 + i_] = 1.0 if j_ >= i_ else 0.0
    c["tri"], c["mask"] = tri, mask
    _CONSTS = c
    return c


_CFG = {"stages": ["ffn1", "in", "att", "pool", "gla", "merge", "ffn2"], "nlayers": DEPTH}


def run_cores(inp, batches, cfg):
    mdl = Model(cfg)
    nc = mdl.build()
    in_maps = [host_inputs(inp, b) for b in batches]
    res = run_bass_kernel_spmd(nc, in_maps, core_ids=list(range(len(batches))))
    if cfg.get("debug"):
        return res.results
    return np.stack([r["out"] for r in res.results], axis=0)


def kernel(**inputs):
    inp = {k: np.asarray(v) for k, v in inputs.items()}
    return run_cores(inp, list(range(8)), _CFG)
```
